# Optimizing a Trainium2 kernel written in Bass

```python
import jax, jax.numpy as jnp
from jax import lax
import numpy as np

D_MODEL = 1024
BATCH = 8
SEQ = 4096
DEPTH = 4

MIX_WIDTH = D_MODEL
HEAD_DIM = 64
NSA_WIDTH = MIX_WIDTH // 2
NSA_HEADS = NSA_WIDTH // HEAD_DIM
NSA_KV_GROUPS = 2
HEADS_PER_GROUP = NSA_HEADS // NSA_KV_GROUPS
KV_WIDTH = NSA_KV_GROUPS * HEAD_DIM
CMP_BLOCK = 32
CMP_STRIDE = 16
CMP_HIDDEN = 128
SEL_BLOCK = 64
N_SEL = 8
WINDOW = 512
Q_BLOCK = 128
POOL_WIDTHS = (2, 4, 8, 16)
POOL_CH = MIX_WIDTH - NSA_WIDTH
POOL_GROUP = POOL_CH // len(POOL_WIDTHS)
N_GATES = 3 * NSA_HEADS
IN_SPLITS = (NSA_WIDTH, KV_WIDTH, KV_WIDTH, KV_WIDTH, KV_WIDTH, KV_WIDTH, KV_WIDTH, N_GATES, POOL_CH)
IN_WIDTH = sum(IN_SPLITS)
D_FF = 2816
EPS = 1e-6

kernel_name = "hymba_style_nsa_pool_macaron"


def rms_norm(x, g):
    xf = x.astype(jnp.float32)
    y = xf * lax.rsqrt(jnp.mean(xf * xf, axis=-1, keepdims=True) + EPS)
    return (y * g.astype(jnp.float32)).astype(x.dtype)


def swiglu(h, wg, wu, wd):
    return (jax.nn.silu(h @ wg) * (h @ wu)) @ wd


def alibi_slopes(n):
    return (2.0 ** (-8.0 * np.arange(1, n + 1) / n)).astype(np.float32)


def masked_softmax(s, mask, axis=-1):
    s = jnp.where(mask, s.astype(jnp.float32), -jnp.inf)
    m = jnp.max(s, axis=axis, keepdims=True)
    m = jnp.where(jnp.isfinite(m), m, 0.0)
    p = jnp.exp(s - m)
    return p / jnp.maximum(jnp.sum(p, axis=axis, keepdims=True), 1e-30)


def compress(kv, pe, w1, w2):
    T = kv.shape[2]
    n_cmp = (T - CMP_BLOCK) // CMP_STRIDE + 1
    idx = np.arange(n_cmp)[:, None] * CMP_STRIDE + np.arange(CMP_BLOCK)[None, :]
    blocks = kv[:, :, idx] + pe
    flat = blocks.reshape(blocks.shape[:3] + (CMP_BLOCK * HEAD_DIM,))
    return jax.nn.gelu(flat @ w1) @ w2


def nsa_mixer(q, k_c, v_c, k_s, v_s, k_w, v_w, gate_logits, pe_k, wk1, wk2, pe_v, wv1, wv2):
    B, T, _ = q.shape
    G, Hg, dh = NSA_KV_GROUPS, HEADS_PER_GROUP, HEAD_DIM
    n_qb = T // Q_BLOCK
    n_cmp = (T - CMP_BLOCK) // CMP_STRIDE + 1
    n_blk = T // SEL_BLOCK
    n_sel = min(N_SEL, n_blk)

    q = q.reshape(B, T, G, Hg, dh).transpose(0, 2, 3, 1, 4) * (dh ** -0.5)
    to_kv = lambda a: a.reshape(B, T, G, dh).transpose(0, 2, 1, 3)
    kc = compress(to_kv(k_c), pe_k, wk1, wk2)
    vc = compress(to_kv(v_c), pe_v, wv1, wv2)
    ks = to_kv(k_s).reshape(B, G, n_blk, SEL_BLOCK, dh)
    vs = to_kv(v_s).reshape(B, G, n_blk, SEL_BLOCK, dh)
    pad = ((0, 0), (0, 0), (WINDOW, 0), (0, 0))
    kw = jnp.pad(to_kv(k_w), pad)
    vw = jnp.pad(to_kv(v_w), pad)
    gates = jax.nn.sigmoid(gate_logits.astype(jnp.float32)).reshape(B, T, G, Hg, 3).transpose(0, 2, 3, 1, 4)

    slopes = jnp.asarray(alibi_slopes(NSA_HEADS)).reshape(G, Hg, 1, 1)
    cmp_pos = jnp.arange(n_cmp) * CMP_STRIDE + (CMP_BLOCK - 1)
    ci = np.arange(n_cmp)[:, None] * CMP_STRIDE
    sj = np.arange(n_blk)[None, :] * SEL_BLOCK
    overlap = jnp.asarray(((ci <= sj + SEL_BLOCK - 1) & (ci + CMP_BLOCK - 1 >= sj)).astype(np.float32))
    blk_ids = jnp.arange(n_blk)
    b_i = jnp.arange(B)[:, None, None, None]
    g_i = jnp.arange(G)[None, :, None, None]

    def one_block(args):
        qb, gb, qi = args
        t = qi * Q_BLOCK + jnp.arange(Q_BLOCK)
        dist_c = t[:, None] - cmp_pos[None, :]
        s_c = jnp.einsum('bghqd,bgcd->bghqc', qb, kc) - slopes * dist_c
        p_c = masked_softmax(s_c, dist_c >= 0)
        o_cmp = jnp.einsum('bghqc,bgcd->bghqd', p_c, vc.astype(jnp.float32))
        imp = jnp.einsum('bghqc,cn->bgqn', p_c, overlap)
        cur = t // SEL_BLOCK
        valid = blk_ids[None, :] <= cur[:, None]
        forced = ((blk_ids[None, :] == 0) | (blk_ids[None, :] == cur[:, None]) |
                  (blk_ids[None, :] == cur[:, None] - 1)) & valid
        imp = jnp.where(forced, jnp.inf, jnp.where(valid, imp, -jnp.inf))
        _, idx = lax.top_k(imp, n_sel)
        k_sel = ks[b_i, g_i, idx]
        v_sel = vs[b_i, g_i, idx]
        pos = idx[..., None] * SEL_BLOCK + jnp.arange(SEL_BLOCK)
        dist_s = t[None, None, :, None, None] - pos
        s_s = jnp.einsum('bghqd,bgqnld->bghqnl', qb, k_sel) - slopes[..., None] * dist_s[:, :, None]
        n_keys = n_sel * SEL_BLOCK
        p_s = masked_softmax(s_s.reshape(B, G, Hg, Q_BLOCK, n_keys),
                             (dist_s >= 0).reshape(B, G, 1, Q_BLOCK, n_keys))
        o_slc = jnp.einsum('bghqk,bgqkd->bghqd', p_s,
                           v_sel.reshape(B, G, Q_BLOCK, n_keys, dh).astype(jnp.float32))
        kwb = lax.dynamic_slice_in_dim(kw, qi * Q_BLOCK, WINDOW + Q_BLOCK, axis=2)
        vwb = lax.dynamic_slice_in_dim(vw, qi * Q_BLOCK, WINDOW + Q_BLOCK, axis=2)
        key_pos = qi * Q_BLOCK - WINDOW + jnp.arange(WINDOW + Q_BLOCK)
        dist_w = t[:, None] - key_pos[None, :]
        mask_w = (dist_w >= 0) & (dist_w < WINDOW) & (key_pos[None, :] >= 0)
        s_w = jnp.einsum('bghqd,bgkd->bghqk', qb, kwb) - slopes * dist_w
        p_w = masked_softmax(s_w, mask_w)
        o_win = jnp.einsum('bghqk,bgkd->bghqd', p_w, vwb.astype(jnp.float32))
        return gb[..., 0:1] * o_cmp + gb[..., 1:2] * o_slc + gb[..., 2:3] * o_win

    qs = q.reshape(B, G, Hg, n_qb, Q_BLOCK, dh).transpose(3, 0, 1, 2, 4, 5)
    gs = gates.reshape(B, G, Hg, n_qb, Q_BLOCK, 3).transpose(3, 0, 1, 2, 4, 5)
    out = lax.map(one_block, (qs, gs, jnp.arange(n_qb)))
    out = out.transpose(1, 0, 4, 2, 3, 5).reshape(B, T, NSA_HEADS * dh)
    return out.astype(q.dtype)


def pool_mixer(u, w_pool, scale):
    B, T, _ = u.shape
    uf = u.astype(jnp.float32)
    c = jnp.pad(jnp.cumsum(uf, axis=1), ((0, 0), (1, 0), (0, 0)))
    t = jnp.arange(T)
    outs = []
    for gi, w in enumerate(POOL_WIDTHS):
        sl = slice(gi * POOL_GROUP, (gi + 1) * POOL_GROUP)
        lo = jnp.maximum(t + 1 - w, 0)
        cnt = jnp.minimum(t + 1, w).astype(jnp.float32)[:, None]
        outs.append((c[:, t + 1, sl] - c[:, lo, sl]) / cnt - uf[..., sl])
    y = jnp.stack(outs, axis=2)
    y = jnp.einsum('btgc,gcd->btgd', y, w_pool.astype(jnp.float32)).reshape(B, T, POOL_CH)
    return (y * scale.astype(jnp.float32)).astype(u.dtype)


def setup_inputs(seed: int = 0) -> dict:
    key = jax.random.key(seed)
    ks = jax.random.split(key, 24)
    n = lambda k, shape, s: jax.random.normal(k, shape, jnp.float32) * s
    gain = lambda k, shape: 1.0 + 0.02 * jax.random.normal(k, shape, jnp.float32)
    L = DEPTH
    return {
        "x": jax.random.normal(ks[0], (BATCH, SEQ, D_MODEL), jnp.float32),
        "ffn1_norm": gain(ks[1], (L, D_MODEL)),
        "ffn1_wg": n(ks[2], (L, D_MODEL, D_FF), D_MODEL ** -0.5),
        "ffn1_wu": n(ks[3], (L, D_MODEL, D_FF), D_MODEL ** -0.5),
        "ffn1_wd": n(ks[4], (L, D_FF, D_MODEL), D_FF ** -0.5),
        "mix_norm": gain(ks[5], (L, D_MODEL)),
        "w_in": n(ks[6], (L, D_MODEL, IN_WIDTH), D_MODEL ** -0.5),
        "cmp_pe_k": n(ks[7], (L, CMP_BLOCK, HEAD_DIM), 0.02),
        "cmp_wk1": n(ks[8], (L, CMP_BLOCK * HEAD_DIM, CMP_HIDDEN), (CMP_BLOCK * HEAD_DIM) ** -0.5),
        "cmp_wk2": n(ks[9], (L, CMP_HIDDEN, HEAD_DIM), CMP_HIDDEN ** -0.5),
        "cmp_pe_v": n(ks[10], (L, CMP_BLOCK, HEAD_DIM), 0.02),
        "cmp_wv1": n(ks[11], (L, CMP_BLOCK * HEAD_DIM, CMP_HIDDEN), (CMP_BLOCK * HEAD_DIM) ** -0.5),
        "cmp_wv2": n(ks[12], (L, CMP_HIDDEN, HEAD_DIM), CMP_HIDDEN ** -0.5),
        "pool_w": n(ks[13], (L, len(POOL_WIDTHS), POOL_GROUP, POOL_GROUP), POOL_GROUP ** -0.5),
        "pool_scale": 1.0 + 0.1 * jax.random.normal(ks[14], (L, POOL_CH), jnp.float32),
        "w_out": n(ks[15], (L, MIX_WIDTH, D_MODEL), MIX_WIDTH ** -0.5),
        "ffn2_norm": gain(ks[16], (L, D_MODEL)),
        "ffn2_wg": n(ks[17], (L, D_MODEL, D_FF), D_MODEL ** -0.5),
        "ffn2_wu": n(ks[18], (L, D_MODEL, D_FF), D_MODEL ** -0.5),
        "ffn2_wd": n(ks[19], (L, D_FF, D_MODEL), D_FF ** -0.5),
        "final_norm": gain(ks[20], (D_MODEL,)),
    }


def reference(x, ffn1_norm, ffn1_wg, ffn1_wu, ffn1_wd, mix_norm, w_in, cmp_pe_k, cmp_wk1, cmp_wk2,
              cmp_pe_v, cmp_wv1, cmp_wv2, pool_w, pool_scale, w_out, ffn2_norm, ffn2_wg, ffn2_wu,
              ffn2_wd, final_norm):
    split_at = list(np.cumsum(IN_SPLITS)[:-1])
    for l in range(DEPTH):
        x = x + 0.5 * swiglu(rms_norm(x, ffn1_norm[l]), ffn1_wg[l], ffn1_wu[l], ffn1_wd[l])
        z = rms_norm(x, mix_norm[l]) @ w_in[l]
        q, k_c, v_c, k_s, v_s, k_w, v_w, g_logit, u = jnp.split(z, split_at, axis=-1)
        o_nsa = nsa_mixer(q, k_c, v_c, k_s, v_s, k_w, v_w, g_logit,
                          cmp_pe_k[l], cmp_wk1[l], cmp_wk2[l], cmp_pe_v[l], cmp_wv1[l], cmp_wv2[l])
        o_pool = pool_mixer(u, pool_w[l], pool_scale[l])
        x = x + jnp.concatenate([o_nsa, o_pool.astype(o_nsa.dtype)], axis=-1).astype(x.dtype) @ w_out[l]
        x = x + 0.5 * swiglu(rms_norm(x, ffn2_norm[l]), ffn2_wg[l], ffn2_wu[l], ffn2_wd[l])
    return rms_norm(x, final_norm)
```

```python
import contextlib
import os
import numpy as np
import ml_dtypes
import concourse.bass as bass
import concourse.mybir as mybir
from concourse.bass_utils import run_bass_kernel_spmd

F32 = mybir.dt.float32
BF16 = mybir.dt.bfloat16
AF = mybir.ActivationFunctionType
ALU = mybir.AluOpType

D = 1024
DFF = 2816
NF = 22
NKC = 8
H = 8
DH = 64
INW = 1816
EPS = 1e-6
BIG = 30000.0
SLOPES = [2.0 ** (-(h + 1)) for h in range(8)]
POOL_W = (2, 4, 8, 16)


class _Op:
    __slots__ = ("eng", "emit", "reads", "writes", "dsem", "deps", "sig", "waits", "signaled")

    def __init__(self, eng, emit, reads, writes, dsem):
        self.eng = eng
        self.emit = emit
        self.reads = reads
        self.writes = writes
        self.dsem = dsem
        self.deps = ()
        self.sig = None
        self.waits = ()
        self.signaled = False


class Sched:
    ENGS = ("pe", "act", "dve", "pool", "sp")
    ROT = 20000

    def __init__(self, nc):
        self.nc = nc
        self.ops = []

    def add(self, eng, emit, reads=(), writes=(), dsem=None):
        self.ops.append(_Op(eng, emit, tuple(reads) + ("PHASE",), tuple(writes), dsem))

    def barrier(self):
        op = _Op("sp", lambda e: e.nop(), (), ("PHASE",), None)
        op.signaled = "barrier"
        self.ops.append(op)

    def analyze(self):
        last_w = {}
        readers = {}
        last_eng = {}
        last_dsem = {}
        for i, op in enumerate(self.ops):
            if op.signaled == "barrier":
                op.deps = set(last_eng.values()) | set(last_dsem.values())
                op.signaled = False
                last_w["PHASE"] = i
                readers["PHASE"] = {}
                last_eng[op.eng] = i
                continue
            if op.dsem is not None:
                last_dsem[op.dsem] = i
            else:
                last_eng[op.eng] = i
            deps = set()
            for r in op.reads:
                w = last_w.get(r)
                if w is not None:
                    deps.add(w)
            for k in op.writes:
                w = last_w.get(k)
                if w is not None:
                    deps.add(w)
                deps.update(readers.get(k, {}).values())
            deps.discard(i)
            rk = ("d", op.dsem) if op.dsem is not None else op.eng
            for r in op.reads:
                readers.setdefault(r, {})[rk] = i
            for k in op.writes:
                last_w[k] = i
                readers[k] = {}
            op.deps = deps
        for op in self.ops:
            for d in op.deps:
                dop = self.ops[d]
                if dop.dsem is None and dop.eng == "pe" and op.eng == "pe" and op.dsem is None:
                    continue
                dop.signaled = True
        eng_cnt = {e: 0 for e in self.ENGS}
        eng_gen = {e: 0 for e in self.ENGS}
        dsem_cnt = {}
        self.sem_names = set()
        known = {e: {} for e in self.ENGS}
        for i, op in enumerate(self.ops):
            waits = {}
            for d in op.deps:
                dop = self.ops[d]
                if dop.dsem is not None:
                    s = ("d", dop.dsem)
                    v = dsem_cnt[dop.dsem]
                else:
                    if dop.eng == "pe" and op.eng == "pe" and op.dsem is None:
                        continue
                    s, v = dop.sig
                if known[op.eng].get(s, 0) >= v:
                    continue
                if waits.get(s, 0) < v:
                    waits[s] = v
            for s, v in waits.items():
                known[op.eng][s] = v
            op.waits = tuple(waits.items())
            if op.dsem is not None:
                dsem_cnt[op.dsem] = dsem_cnt.get(op.dsem, 0) + 16
                op.sig = (("d", op.dsem), dsem_cnt[op.dsem])
                self.sem_names.add(("d", op.dsem))
            elif op.signaled:
                e = op.eng
                if eng_cnt[e] >= self.ROT:
                    eng_gen[e] += 1
                    eng_cnt[e] = 0
                eng_cnt[e] += 1
                s = ("e", e, eng_gen[e])
                op.sig = (s, eng_cnt[e])
                self.sem_names.add(s)
        self.dsem_tot = dsem_cnt

    def emit(self, final_waits=()):
        nc = self.nc
        self.analyze()
        sems = {}
        with contextlib.ExitStack() as es:
            for n, s in enumerate(sorted(self.sem_names, key=str)):
                sems[s] = es.enter_context(nc.semaphore("sm%d" % n))
            block = es.enter_context(nc.Block())
            by_eng = {e: [op for op in self.ops if op.eng == e] for e in self.ENGS}

            def run(engobj, ops):
                for op in ops:
                    for s, v in op.waits:
                        engobj.wait_ge(sems[s], v)
                    ins = op.emit(engobj)
                    if op.dsem is not None:
                        ins.then_inc(sems[op.sig[0]], 16)
                    elif op.signaled:
                        ins.then_inc(sems[op.sig[0]], 1)

            @block.tensor
            def _(e):
                run(e, by_eng["pe"])

            @block.scalar
            def _(e):
                run(e, by_eng["act"])

            @block.vector
            def _(e):
                run(e, by_eng["dve"])

            @block.gpsimd
            def _(e):
                run(e, by_eng["pool"])

            @block.sync
            def _(e):
                run(e, by_eng["sp"])
                for key in final_waits:
                    e.wait_ge(sems[("d", key)], self.dsem_tot[key])


class Arena:
    def __init__(self, nc, nbytes):
        self.t = nc.alloc_sbuf_tensor("arena", [128, nbytes // 4], F32)
        self.cap = nbytes
        self.off = 0
        self.n = 0

    def alloc(self, shape, dtype):
        esz = 2 if dtype == BF16 else 4
        n = int(np.prod(shape))
        nb = (n * esz + 63) // 64 * 64
        if self.off + nb > self.cap:
            raise RuntimeError("arena overflow: need %d have %d" % (self.off + nb, self.cap))
        v = self.t[:, self.off // 4:(self.off + nb) // 4]
        self.off += nb
        if dtype == BF16:
            v = v.bitcast(BF16)
        v = v[:, 0:n]
        if len(shape) == 2:
            v = v.rearrange("p (a b) -> p a b", b=shape[1])
        elif len(shape) == 3:
            v = v.rearrange("p (a b c) -> p a b c", b=shape[1], c=shape[2])
        return v


class Ctx:
    pass


def build(T, L, with_mixer=True):
    nc = bass.Bass("TRN2", target_bir_lowering=False)
    S = Sched(nc)
    c = Ctx()
    c.nc, c.S, c.T, c.L = nc, S, T, L
    NG = T // 512
    c.NG = NG

    def din(name, shape, dt=F32):
        return nc.dram_tensor(name, list(shape), dt, kind="ExternalInput").ap()

    c.x_in = din("x", [T, D])
    c.wgu = din("wgu", [L, 2, NF, 128, 2 * NKC * 128])
    c.wd = din("wd", [L, 2, NKC, 128, NF * 128])
    c.gam = din("gam", [128, L * 24 + 8])
    c.cident = din("cident", [128, 128])
    if with_mixer:
        declare_mixer_inputs(c, din)
    c.out = nc.dram_tensor("out", [T, D], F32, kind="ExternalOutput").ap()
    c.xT = nc.dram_tensor("xT", [D, T], F32, kind="Internal").ap()

    c.A = Arena(nc, 207 * 1024)
    c.PS = [nc.alloc_psum_tensor("psb%d" % i, [128, 512], F32)[:] for i in range(8)]
    A = c.A
    c.ident32 = A.alloc((128,), F32)
    c.identb = A.alloc((128,), BF16)
    c.onesb = A.alloc((128,), BF16)
    c.gamt = A.alloc((L * 24 + 8,), F32)
    S.add("sp", lambda e: e.dma_start(out=c.ident32, in_=c.cident), writes=["ident32"], dsem="c0")
    S.add("sp", lambda e: e.dma_start(out=c.gamt, in_=c.gam), writes=["gam"], dsem="c1")
    S.add("dve", lambda e: e.tensor_copy(out=c.identb, in_=c.ident32), reads=["ident32"], writes=["identb"])
    S.add("dve", lambda e: e.memset(c.onesb, 1.0), writes=["onesb"])
    c.tinyt = A.alloc((1,), F32)
    S.add("dve", lambda e: e.memset(c.tinyt, 1e-30), writes=["tinyt"])
    c.epst = A.alloc((1,), F32)
    S.add("dve", lambda e: e.memset(c.epst, EPS), writes=["epst"])
    c.base = A.off

    phase_in(c)
    for l in range(L):
        phase_ffn(c, l, 0)
        if with_mixer:
            phase_mixer(c, l)
        phase_ffn(c, l, 1)
    phase_out(c)
    S.emit(final_waits=["out"])
    return nc


def norm_stats(c, xs, cols, sq, stat_ps, rstd, kx, tag, pskey=None, kxs=None):
    S = c.S
    if pskey is None:
        pskey = ("ps", tag)
    for ch in range(NKC):
        sqt = sq[ch % 2]
        S.add("act", lambda e, sqt=sqt, ch=ch: e.activation(out=sqt, in_=xs[:, ch, cols], func=AF.Square),
              reads=[kx if kxs is None else kxs[ch]], writes=[("sq", tag, ch % 2)])
        S.add("pe", lambda e, sqt=sqt, ch=ch: e.matmul(stat_ps, lhsT=c.onesb, rhs=sqt, start=(ch == 0), stop=(ch == NKC - 1)),
              reads=[("sq", tag, ch % 2), "onesb"], writes=[pskey])
    S.add("act", lambda e: e.activation(out=rstd, in_=stat_ps, func=AF.Ln, scale=1.0 / D, bias=c.epst),
          reads=[pskey, "epst"], writes=[("rstd", tag)])
    S.add("act", lambda e: e.activation(out=rstd, in_=rstd, func=AF.Exp, scale=-0.5),
          reads=[("rstd", tag)], writes=[("rstd", tag)])


def phase_in(c):
    S, A, nc = c.S, c.A, c.nc
    S.barrier()
    A.off = c.base
    NT = c.T // 128
    xin = [A.alloc((D,), F32) for _ in range(2)]
    xo = [A.alloc((NKC, 128), F32) for _ in range(2)]
    for tt in range(NT):
        b = tt % 2
        S.add("sp", lambda e, tt=tt, b=b: e.dma_start(out=xin[b], in_=c.x_in[tt * 128:(tt + 1) * 128, :]),
              writes=[("xin", b)], dsem=("xin", b))
        for half in range(2):
            ps = c.PS[(tt * 2 + half) % 4]
            for q in range(4):
                ch = half * 4 + q
                S.add("pe", lambda e, ps=ps, q=q, ch=ch, b=b: e.transpose(out=ps[:, q * 128:(q + 1) * 128], in_=xin[b][:, ch * 128:(ch + 1) * 128], identity=c.ident32),
                      reads=[("xin", b), "ident32"], writes=[("ps", (tt * 2 + half) % 4)])
            eng = "act" if half == 0 else "dve"
            dst = xo[b][:, half * 4:(half + 1) * 4, :]
            src = ps.rearrange("p (a b) -> p a b", b=128)
            if eng == "act":
                S.add("act", lambda e, dst=dst, src=src: e.activation(out=dst, in_=src, func=AF.Copy),
                      reads=[("ps", (tt * 2 + half) % 4)], writes=[("xo", b, half)])
            else:
                S.add("dve", lambda e, dst=dst, src=src: e.tensor_copy(out=dst, in_=src),
                      reads=[("ps", (tt * 2 + half) % 4)], writes=[("xo", b, half)])
        S.add("sp", lambda e, tt=tt, b=b: e.dma_start(out=c.xT.rearrange("(c p) t -> p c t", p=128)[:, :, tt * 128:(tt + 1) * 128], in_=xo[b]),
              reads=[("xo", b, 0), ("xo", b, 1)], writes=[("xT", tt)], dsem=("xo", b))


def phase_out(c):
    S, A, nc = c.S, c.A, c.nc
    S.barrier()
    A.off = c.base
    NG = c.NG
    gcol = c.L * 24
    xs = [A.alloc((NKC, 512), F32) for _ in range(2)]
    xn = A.alloc((NKC, 512), F32)
    sq = [A.alloc((512,), BF16) for _ in range(2)]
    rstd = A.alloc((512,), F32)
    yo = [A.alloc((D,), F32) for _ in range(2)]
    xTv = c.xT.rearrange("(c p) t -> p c t", p=128)
    for g in range(NG):
        b = g % 2
        S.add("sp", lambda e, g=g, b=b: e.dma_start(out=xs[b], in_=xTv[:, :, g * 512:(g + 1) * 512]),
              reads=[("xT", g * 4 + i) for i in range(4)], writes=[("xs", b)], dsem=("xs", b))
        norm_stats(c, xs[b], slice(0, 512), sq, c.PS[7], rstd, ("xs", b), "fo")
        for ch in range(NKC):
            S.add("dve", lambda e, ch=ch, b=b: e.scalar_tensor_tensor(out=xn[:, ch, :], in0=xs[b][:, ch, :], scalar=c.gamt[:, gcol + ch:gcol + ch + 1], in1=rstd, op0=ALU.mult, op1=ALU.mult),
                  reads=[("xs", b), ("rstd", "fo"), "gam"], writes=[("xno", ch)])
        for tt in range(4):
            ob = (g * 4 + tt) % 2
            for half in range(2):
                pk = (g * 8 + tt * 2 + half) % 4
                ps = c.PS[pk]
                for q in range(4):
                    ch = half * 4 + q
                    S.add("pe", lambda e, ps=ps, q=q, ch=ch, tt=tt: e.transpose(out=ps[:, q * 128:(q + 1) * 128], in_=xn[:, ch, tt * 128:(tt + 1) * 128], identity=c.ident32),
                          reads=[("xno", ch), "ident32"], writes=[("ps", pk)])
                dst = yo[ob][:, half * 512:(half + 1) * 512]
                if half == 0:
                    S.add("act", lambda e, dst=dst, ps=ps: e.activation(out=dst, in_=ps, func=AF.Copy),
                          reads=[("ps", pk)], writes=[("yo", ob, half)])
                else:
                    S.add("dve", lambda e, dst=dst, ps=ps: e.tensor_copy(out=dst, in_=ps),
                          reads=[("ps", pk)], writes=[("yo", ob, half)])
            t0 = g * 512 + tt * 128
            S.add("sp", lambda e, t0=t0, ob=ob: e.dma_start(out=c.out[t0:t0 + 128, :], in_=yo[ob]),
                  reads=[("yo", ob, 0), ("yo", ob, 1)], dsem="out")


def phase_ffn(c, l, which):
    S, A, nc, T = c.S, c.A, c.nc, c.T
    S.barrier()
    A.off = c.base
    NP = 1024 if T >= 1024 else 512
    NS = NP // 512
    npass = T // NP
    gcol = l * 24 + (0 if which == 0 else 16)
    tagp = ("f", l, which)
    xsb = [A.alloc((NKC, NP), F32) for _ in range(2)]
    xn = A.alloc((NKC, NP), BF16)
    hT = A.alloc((NF, NP), BF16)
    sq = [A.alloc((512,), BF16) for _ in range(2)]
    rstd = A.alloc((512,), F32)
    sg = [A.alloc((512,), F32) for _ in range(2)]
    RGU, RD = 6, 4
    gu_slots = [A.alloc((2, NKC, 128), BF16) for _ in range(RGU)]
    d_slots = [A.alloc((NF, 128), BF16) for _ in range(RD)]
    xTv = c.xT.rearrange("(c p) t -> p c t", p=128)

    gu_loads = [(ps_, f) for ps_ in range(npass) for f in range(NF)]
    d_loads = [(ps_, o) for ps_ in range(npass) for o in range(NKC)]
    st = {"gu": 0, "d": 0}

    def issue_gu(upto):
        while st["gu"] < min(upto, len(gu_loads)):
            i = st["gu"]
            _, f = gu_loads[i]
            slot = i % RGU
            S.add("pool", lambda e, f=f, slot=slot: e.dma_start(out=gu_slots[slot], in_=c.wgu[l, which, f].rearrange("p (a b c) -> p a b c", a=2, b=NKC)),
                  writes=[("gu", slot)], dsem=("gu", slot))
            st["gu"] += 1

    def issue_d(upto):
        while st["d"] < min(upto, len(d_loads)):
            i = st["d"]
            _, o = d_loads[i]
            slot = i % RD
            S.add("pool", lambda e, o=o, slot=slot: e.dma_start(out=d_slots[slot], in_=c.wd[l, which, o].rearrange("p (a b) -> p a b", a=NF)),
                  writes=[("wd", slot)], dsem=("wd", slot))
            st["d"] += 1

    issue_gu(RGU)
    issue_d(RD - 1)
    gi = 0
    di = 0
    def load_x(p_):
        t0 = p_ * NP
        xb = p_ % 2
        S.add("sp", lambda e, t0=t0, xs=xsb[xb]: e.dma_start(out=xs, in_=xTv[:, :, t0:t0 + NP]),
              reads=[("xT", i) for i in range(t0 // 128, (t0 + NP) // 128)], writes=[("xs", xb)], dsem=("xs", xb))

    def do_norm(p_, s):
        xb = p_ % 2
        xs = xsb[xb]
        kxs = ("xs", xb)
        cols = slice(s * 512, (s + 1) * 512)
        norm_stats(c, xs, cols, sq, c.PS[6], rstd, kxs, "ff")
        for ch in range(NKC):
            S.add("dve", lambda e, ch=ch, cols=cols, xs=xs: e.scalar_tensor_tensor(out=xn[:, ch, cols], in0=xs[:, ch, cols], scalar=c.gamt[:, gcol + ch:gcol + ch + 1], in1=rstd, op0=ALU.mult, op1=ALU.mult),
                  reads=[kxs, ("rstd", "ff"), "gam"], writes=[("xn", s)])

    load_x(0)
    for s in range(NS):
        do_norm(0, s)
    for p_ in range(npass):
        t0 = p_ * NP
        xb = p_ % 2
        xs = xsb[xb]
        kxs = ("xs", xb)
        tkeys = [("xT", i) for i in range(t0 // 128, (t0 + NP) // 128)]
        if p_ + 1 < npass:
            load_x(p_ + 1)
        for f in range(NF):
            slot = gi % RGU
            for s in range(NS):
                cols = slice(s * 512, (s + 1) * 512)
                k = (f * NS + s) % 2
                pg, pu = c.PS[k], c.PS[2 + k]
                for ch in range(NKC):
                    S.add("pe", lambda e, pg=pg, ch=ch, slot=slot, cols=cols: e.matmul(pg, lhsT=gu_slots[slot][:, 0, ch, :], rhs=xn[:, ch, cols], start=(ch == 0), stop=(ch == NKC - 1)),
                          reads=[("gu", slot), ("xn", s)], writes=[("ps", k)])
                for ch in range(NKC):
                    S.add("pe", lambda e, pu=pu, ch=ch, slot=slot, cols=cols: e.matmul(pu, lhsT=gu_slots[slot][:, 1, ch, :], rhs=xn[:, ch, cols], start=(ch == 0), stop=(ch == NKC - 1)),
                          reads=[("gu", slot), ("xn", s)], writes=[("ps", 2 + k)])
                S.add("act", lambda e, pg=pg, k=k: e.activation(out=sg[k], in_=pg, func=AF.Silu),
                      reads=[("ps", k)], writes=[("sg", k)])
                S.add("dve", lambda e, pu=pu, k=k, f=f, cols=cols: e.tensor_tensor(out=hT[:, f, cols], in0=pu, in1=sg[k], op=ALU.mult),
                      reads=[("ps", 2 + k), ("sg", k)], writes=[("hT", f, s)])
            gi += 1
            issue_gu(gi + RGU)
        for o in range(NKC):
            slot = di % RD
            for s in range(NS):
                cols = slice(s * 512, (s + 1) * 512)
                k = 4 + (o * NS + s) % 2
                py = c.PS[k]
                for f in range(NF):
                    S.add("pe", lambda e, py=py, f=f, slot=slot, cols=cols: e.matmul(py, lhsT=d_slots[slot][:, f, :], rhs=hT[:, f, cols], start=(f == 0), stop=(f == NF - 1)),
                          reads=[("wd", slot), ("hT", f, s)], writes=[("ps", k)])
                S.add("dve", lambda e, py=py, o=o, cols=cols, xs=xs: e.scalar_tensor_tensor(out=xs[:, o, cols], in0=py, scalar=0.5, in1=xs[:, o, cols], op0=ALU.mult, op1=ALU.add),
                      reads=[("ps", k), kxs], writes=[("xso", xb, o)])
            di += 1
            issue_d(di + RD - 1)
            if p_ + 1 < npass and 1 <= o <= NS:
                do_norm(p_ + 1, o - 1)
        S.add("sp", lambda e, t0=t0, xs=xs: e.dma_start(out=xTv[:, :, t0:t0 + NP], in_=xs),
              reads=[kxs] + [("xso", xb, o) for o in range(NKC)], writes=tkeys, dsem="xsw")


def declare_mixer_inputs(c, din):
    L, T = c.L, c.T
    NT, NG = T // 128, T // 512
    NCC = (NG + 3) // 4
    c.NCC = NCC
    c.win = din("win", [L, 128, NKC * INW])
    c.wout = din("wout", [L, 128, NKC * D])
    c.w1 = din("w1", [L, 128, 32 * 128])
    c.w2 = din("w2", [L, 128, 128])
    c.pet = din("pet", [L, 128, 32])
    c.pw = din("pw", [L, 128, 512])
    c.psc = din("psc", [L, 128, 4])
    c.ckrow = din("ckrow", [64, T])
    c.cmask = din("cmask", [12, 128, 512])
    c.cab = din("cab", [128, 8 * NT])
    c.cabc = din("cabc", [128, 8 * NCC * NG])
    c.cjr = din("cjr", [1, 512])
    c.cfm = din("cfm", [128, NT * 64])
    c.cov = din("cov", [128, NCC * 65])
    c.csel = din("csel", [24, 24 * 64])
    c.cic0 = din("cic0", [128, 64])


def phase_mixer(c, l):
    S, A, nc, T = c.S, c.A, c.nc, c.T
    S.barrier()
    A.off = c.base
    NT, NG, NCC = T // 128, T // 512, c.NCC
    gcol = l * 24 + 8
    PS = c.PS
    WIN = A.alloc((NKC, INW), BF16)
    WO = A.alloc((NKC, D), BF16)
    W1 = A.alloc((32, 128), BF16)
    W2 = A.alloc((2, 64), BF16)
    PET = A.alloc((32,), BF16)
    PW = A.alloc((4, 128), BF16)
    PSC = A.alloc((4,), F32)
    CB = A.alloc((2,), F32)
    KS = A.alloc((2, T), BF16)
    KW = A.alloc((2, 1024), BF16)
    KCC = A.alloc((2, 128 * NCC), BF16)
    VS = A.alloc((NT, 2, 128), BF16)
    VW = A.alloc((8, 2, 128), BF16)
    VCA = A.alloc((NCC, 2, 128), BF16)
    HK = A.alloc((2, 128 * NCC), BF16)
    HV = A.alloc((2, 128 * NCC), BF16)
    KCT = A.alloc((2, 528), BF16)
    xs = A.alloc((NKC, 512), F32)
    xn = A.alloc((NKC, 512), BF16)
    sq = [A.alloc((512,), BF16) for _ in range(2)]
    rstd = A.alloc((512,), F32)
    QA = A.alloc((8, 512), BF16)
    ON = A.alloc((4, 512), BF16)
    OPt = A.alloc((4, 512), BF16)
    NPT = 4
    PT = [A.alloc((512,), BF16) for _ in range(NPT)]
    U = A.alloc((4, 528), F32)
    UW = [A.alloc((2, 528), F32), A.alloc((2, 528), F32)]
    Y2 = [A.alloc((512,), BF16) for _ in range(2)]
    SG = A.alloc((512,), BF16)
    IA2 = [A.alloc((4, 64), F32) for _ in range(2)]
    IMX = A.alloc((4, 8), F32)
    RL = A.alloc((4,), F32)
    SM2 = [A.alloc((4, 128), BF16) for _ in range(2)]
    LM = A.alloc((512,), F32)
    GR = A.alloc((512,), F32)
    TMP = LM
    ACC = A.alloc((4, 512), F32)
    GX = A.alloc((4, 32), F32)
    GT = A.alloc((4, 32), F32)
    MSK = A.alloc((12, 512), BF16)
    AB = A.alloc((8, NT), F32)
    ABC = A.alloc((8, NCC, NG), F32)
    JR = A.alloc((512,), F32)
    FMg = A.alloc((4, 64), F32)
    OV = A.alloc((NCC, 65), BF16)
    SEL = A.alloc((24, 64), BF16)
    IC0 = A.alloc((4, 16), F32)

    def dve(fn, reads, writes):
        S.add("dve", fn, reads=reads, writes=writes)

    def act(fn, reads, writes):
        S.add("act", fn, reads=reads, writes=writes)

    def pool(fn, reads, writes):
        S.add("pool", fn, reads=reads, writes=writes)

    def pe(fn, reads, writes):
        S.add("pe", fn, reads=reads, writes=writes)

    ld = {"n": 0}

    def load_cast(dst, src, key):
        n = ld["n"]
        ld["n"] += 1
        S.add("pool", lambda e: e.dma_start(out=dst, in_=src), writes=[key], dsem=("mld", n % 8))

    def load_sp(dst, src, key):
        n = ld["n"]
        ld["n"] += 1
        S.add("sp", lambda e: e.dma_start(out=dst, in_=src), writes=[key], dsem=("mls", n % 8))

    winv = c.win[l].rearrange("p (a b) -> p a b", a=NKC)
    for ch in range(NKC):
        load_cast(WIN[:, ch, :], winv[:, ch, :], ("WIN", ch))
    load_cast(W1[:, 0:16, :], c.w1[l].rearrange("p (a b) -> p a b", b=128)[:, 0:16, :], ("W1", 0))
    load_cast(W1[:, 16:32, :], c.w1[l].rearrange("p (a b) -> p a b", b=128)[:, 16:32, :], ("W1", 1))
    load_cast(W2, c.w2[l].rearrange("p (a b) -> p a b", b=64), "W2")
    load_cast(PET, c.pet[l], "PET")
    load_cast(PW, c.pw[l].rearrange("p (a b) -> p a b", b=128), "PW")
    load_sp(PSC, c.psc[l], "PSC")
    for g in range(2):
        load_cast(KS[64:128, g, :], c.ckrow, ("KSc", g))
    load_cast(MSK, c.cmask.rearrange("m p j -> p m j"), "MSK")
    load_sp(AB, c.cab.rearrange("p (a b) -> p a b", a=8), "AB")
    load_sp(ABC, c.cabc.rearrange("p (a b c) -> p a b c", a=8, b=NCC), "ABC")
    load_sp(JR[64:65, :], c.cjr, "JR")
    load_cast(OV, c.cov.rearrange("p (a b) -> p a b", b=65), "OV")
    load_cast(SEL[0:24], c.csel.rearrange("p (a b) -> p a b", b=64), "SEL")
    load_sp(IC0, c.cic0.rearrange("p (a b) -> p a b", b=16), "IC0")
    wov = c.wout[l].rearrange("p (a b) -> p a b", a=NKC)
    for ch in range(NKC):
        load_cast(WO[:, ch, :], wov[:, ch, :], ("WO", ch))
    WINK = [("WIN", ch) for ch in range(NKC)]
    WOK = [("WO", ch) for ch in range(NKC)]
    pool(lambda e: e.memset(KW[64:128], 0.0), [], ["KWc"])
    pool(lambda e: e.memset(KW[64:65], 1.0), ["KWc"], ["KWc"])
    pool(lambda e: e.memset(KCC, 0.0), [], ["KCC"])
    pool(lambda e: e.memset(KCC[64:65], 1.0), ["KCC"], ["KCC"])
    pool(lambda e: e.memset(HK, 0.0), [], ["HK"])
    pool(lambda e: e.memset(HV, 0.0), [], ["HV"])
    pool(lambda e: e.memset(VS[:, :, :, 64:128], 1.0), [], ["VSc"])
    pool(lambda e: e.memset(VW[:, :, :, 64:128], 1.0), [], ["VWc"])
    pool(lambda e: e.memset(VCA, 0.0), [], ["VCA"])
    pool(lambda e: e.memset(VCA[:, :, :, 64:128], 1.0), ["VCA"], ["VCA"])
    pool(lambda e: e.memset(U, 0.0), [], ["U"])
    pool(lambda e: e.memset(KCT, 0.0), [], ["KCT"])
    pool(lambda e: e.memset(SM2[0], 0.0), [], [("SM", 0)])
    pool(lambda e: e.memset(SM2[1], 0.0), [], [("SM", 1)])
    pool(lambda e: e.memset(QA, 0.0), [], [("QA", h, x) for h in range(8) for x in "qm"])
    for h in range(8):
        pool(lambda e, h=h: e.tensor_scalar(out=QA[64:65, h, :], in0=JR[64:65, :], scalar1=-SLOPES[h], scalar2=None, op0=ALU.mult),
             ["JR", ("QA", h, "m")], [("QA", h, "m")])
    for kv in range(2):
        rows = slice(0, 64) if kv == 0 else slice(64, 128)
        for li in range(32):
            pe(lambda e, kv=kv, rows=rows, li=li: e.matmul(PS[6 - kv][:, 0:1], lhsT=W1[rows, li, :], rhs=PET[rows, li:li + 1], start=(li == 0), stop=(li == 31)),
               [("W1", li // 16), "PET"], [("ps", 6 - kv)])
    for kv in range(2):
        dve(lambda e, kv=kv: e.tensor_copy(out=CB[:, kv:kv + 1], in_=PS[6 - kv][:, 0:1]), [("ps", 6 - kv)], ["CB"])

    xTv = c.xT.rearrange("(c p) t -> p c t", p=128)
    wbs = [0, 1, 7]
    st = {"wb": 0, "pt": 0, "ev": 0}

    def next_wb():
        k = wbs[st["wb"] % 3]
        st["wb"] += 1
        return k

    def evac(fn_act, fn_dve, reads, writes):
        if os.environ.get("EVDVE"):
            dve(fn_dve, reads, writes)
        elif st["wb"] % 2 == 0:
            act(fn_act, reads, writes)
        else:
            dve(fn_dve, reads, writes)

    def proj(lhs_fn, M, rkeys):
        k = next_wb()
        for ch in range(NKC):
            pe(lambda e, ch=ch, k=k: e.matmul(PS[k][0:M, :], lhsT=lhs_fn(ch), rhs=xn[:, ch, :], start=(ch == 0), stop=(ch == NKC - 1)),
               [("WIN", ch), "xn"], [("ps", k)])
        return k

    abk = {"n": 0}
    pend = []
    LAGG = 3

    def pipe_pop():
        t, k, O, okey, first, last, done_cb = pend.pop(0)
        p = st["pt"] % NPT
        st["pt"] += 1
        act(lambda e, t=t, k=k, p=p: e.activation(out=PT[p], in_=PS[k], func=AF.Exp, bias=t["bias"]),
            [("ps", k)] + t["bk"], [("PT", p)])
        pe(lambda e, t=t, p=p: e.matmul(O, lhsT=t["v"], rhs=PT[p], start=first, stop=last),
           [("PT", p)] + t["vk"], [okey])
        if t.get("extra") is not None:
            t["extra"](PT[p], ("PT", p), first, last)
        if last and done_cb is not None:
            done_cb()

    def pipe_flush():
        while pend:
            pipe_pop()

    def attn(tiles, O, okey, banks=(0, 1, 7), done_cb=None):
        n = len(tiles)
        for i, t in enumerate(tiles):
            k = banks[abk["n"] % len(banks)]
            abk["n"] += 1
            pe(lambda e, t=t, k=k: e.matmul(PS[k], lhsT=t["lhsT"], rhs=t["rhs"], start=True, stop=(t["mask"] is None)),
               t["rk"], [("ps", k)])
            if t["mask"] is not None:
                pe(lambda e, t=t, k=k: e.matmul(PS[k], lhsT=c.identb, rhs=t["mask"], start=False, stop=True),
                   ["identb", "MSK"], [("ps", k)])
            pend.append((t, k, O, okey, i == 0, i == n - 1, done_cb))
            while len(pend) > min(LAGG, len(banks) - 1):
                pipe_pop()

    def gate_combine(h, br, Ops, okey, mode, hh):
        pe(lambda e: e.matmul(PS[6][0:64, :], lhsT=SEL[0:24, 3 * h + br, :], rhs=SG[0:24, :], start=True, stop=True),
           ["SEL", "SG"], [("ps", 6)])
        act(lambda e: e.activation(out=LM[0:64, :], in_=Ops[64:128, :], func=AF.Ln, bias=c.tinyt[0:64, :]), [okey, "tinyt"], ["LM"])
        act(lambda e: e.activation(out=LM[0:64, :], in_=LM[0:64, :], func=AF.Exp, scale=-1.0), ["LM"], ["LM"])
        dve(lambda e: e.tensor_tensor(out=GR[0:64, :], in0=PS[6][0:64, :], in1=LM[0:64, :], op=ALU.mult),
            [("ps", 6), "LM"], ["GR"])
        pb = 64 * (hh // 4)
        hq = hh % 4
        accv = ACC[pb:pb + 64, hq, :]
        tmpv = TMP[pb:pb + 64, :]
        if mode == 0:
            dve(lambda e: e.tensor_tensor(out=accv, in0=Ops[0:64, :], in1=GR[0:64, :], op=ALU.mult),
                [okey, "GR"], [("ACC", hh)])
        else:
            dve(lambda e: e.tensor_tensor(out=tmpv, in0=Ops[0:64, :], in1=GR[0:64, :], op=ALU.mult),
                [okey, "GR", "LM"], ["LM"])
            if mode == 1:
                dve(lambda e: e.tensor_tensor(out=accv, in0=accv, in1=tmpv, op=ALU.add),
                    [("ACC", hh), "LM"], [("ACC", hh)])
            else:
                r0 = (h % 2) * 64
                dve(lambda e: e.tensor_tensor(out=ON[r0:r0 + 64, h // 2, :], in0=accv, in1=tmpv, op=ALU.add),
                    [("ACC", hh), "LM"], [("ON", h)])

    import os
    STOP = int(os.environ.get("MIXSTOP", "9"))
    SUB = os.environ.get("MIXSUB", "q,c,s,w,g,u,v").split(",")
    for G in range(NG):
        t0 = G * 512
        tkeys = [("xT", i) for i in range(t0 // 128, t0 // 128 + 4)]
        for ch in range(NKC):
            S.add("sp", lambda e, t0=t0, ch=ch: e.dma_start(out=xs[:, ch, :], in_=xTv[:, ch, t0:t0 + 512]), reads=tkeys, writes=[("xs", ch)], dsem=("mxs", ch))
        XSK = [("xs", ch) for ch in range(NKC)]
        if STOP < 1:
            S.add("sp", lambda e, t0=t0: e.dma_start(out=xTv[:, :, t0:t0 + 512], in_=xs), reads=XSK, writes=tkeys, dsem="mxwd")
            continue
        S.add("sp", lambda e, G=G: e.dma_start(out=FMg, in_=c.cfm[:, G * 256:(G + 1) * 256].rearrange("p (a b) -> p a b", b=64)), writes=["FMg"], dsem="mfm")
        norm_stats(c, xs, slice(0, 512), sq, PS[6], rstd, None, "mx", pskey=("ps", 6), kxs=XSK)
        for ch in range(NKC):
            dve(lambda e, ch=ch: e.scalar_tensor_tensor(out=xn[:, ch, :], in0=xs[:, ch, :], scalar=c.gamt[:, gcol + ch:gcol + ch + 1], in1=rstd, op0=ALU.mult, op1=ALU.mult),
                [("xs", ch), ("rstd", "mx"), "gam"], ["xn"])
        for j in range(4 if "q" in SUB else 0):
            k = proj(lambda ch, j=j: WIN[:, ch, j * 128:(j + 1) * 128], 128, None)
            for hp in range(2):
                h = 2 * j + hp
                src = PS[k][hp * 64:(hp + 1) * 64, :]
                evac(lambda e, h=h, src=src: e.activation(out=QA[0:64, h, :], in_=src, func=AF.Copy, scale=0.125),
                     lambda e, h=h, src=src: e.tensor_scalar(out=QA[0:64, h, :], in0=src, scalar1=0.125, scalar2=None, op0=ALU.mult),
                     [("ps", k)], [("QA", h, "q")])
        if G > 0:
            pool(lambda e: e.tensor_copy(out=KCT.rearrange("p g (s i) -> p g s i", i=33)[:, :, :, 0], in_=KCT.rearrange("p g (s i) -> p g s i", i=33)[:, :, :, 32]), ["KCT"], ["KCT"])
        for g in range(2 if "c" in SUB else 0):
            k = proj(lambda ch, g=g: WIN[:, ch, 512 + 128 * g:640 + 128 * g], 128, None)
            kdst = KCT[:, g, :].rearrange("p (s i) -> p i s", i=33)[:, 1:33, :]
            evac(lambda e, kdst=kdst, k=k: e.activation(out=kdst, in_=PS[k].rearrange("p (i s) -> p i s", s=16), func=AF.Copy),
                 lambda e, kdst=kdst, k=k: e.tensor_copy(out=kdst, in_=PS[k].rearrange("p (i s) -> p i s", s=16)),
                 [("ps", k), "KCT"], ["KCT"])
        k = proj(lambda ch: WIN[:, ch, 768:896], 128, None)
        for g in range(2 if "s" in SUB else 0):
            src = PS[k][g * 64:(g + 1) * 64, :]
            evac(lambda e, g=g, src=src, t0=t0: e.activation(out=KS[0:64, g, t0:t0 + 512], in_=src, func=AF.Copy),
                 lambda e, g=g, src=src, t0=t0: e.tensor_copy(out=KS[0:64, g, t0:t0 + 512], in_=src),
                 [("ps", k)], [("KS", g, G)])
        k = proj(lambda ch: WIN[:, ch, 1152:1280], 128, None)
        wc0 = (G % 2) * 512
        for g in range(2 if "w" in SUB else 0):
            src = PS[k][g * 64:(g + 1) * 64, :]
            evac(lambda e, g=g, src=src, wc0=wc0: e.activation(out=KW[0:64, g, wc0:wc0 + 512], in_=src, func=AF.Copy),
                 lambda e, g=g, src=src, wc0=wc0: e.tensor_copy(out=KW[0:64, g, wc0:wc0 + 512], in_=src),
                 [("ps", k)], [("KW", g, G % 2)])
        k = proj(lambda ch: WIN[:, ch, 1280:1304], 24, None) if "g" in SUB else next_wb()
        if "g" in SUB:
          act(lambda e, k=k: e.activation(out=SG[0:24, :], in_=PS[k][0:24, :], func=AF.Sigmoid), [("ps", k)], ["SG"])
        if G > 0:
            pool(lambda e: e.tensor_copy(out=U[:, :, 0:16], in_=U[:, :, 512:528]), ["U"], ["U"])
        for j in range(4 if "u" in SUB else 0):
            k = proj(lambda ch, j=j: WIN[:, ch, 1304 + 128 * j:1304 + 128 * (j + 1)], 128, None)
            evac(lambda e, j=j, k=k: e.activation(out=U[:, j, 16:528], in_=PS[k], func=AF.Copy),
                 lambda e, j=j, k=k: e.tensor_copy(out=U[:, j, 16:528], in_=PS[k]),
                 [("ps", k), "U"], ["U"])
        for tt in range(4 if ("v" in SUB or "vm" in SUB or "v1" in SUB or "v2" in SUB) else 0):
            k = next_wb()
            kc = 4 * G + tt
            for ch in range(NKC):
                pe(lambda e, ch=ch, k=k, tt=tt: e.matmul(PS[k][:, 0:256], lhsT=xn[:, ch, tt * 128:(tt + 1) * 128],
                                                          rhs=WIN[:, ch, 896:1152],
                                                          start=(ch == 0), stop=(ch == NKC - 1)),
                   [("WIN", ch), "xn"], [("ps", k)])
            if "vm" in SUB or "v2" in SUB:
                pass
            else:
              evac(lambda e, k=k, kc=kc: e.activation(out=VS[:, kc, :, 0:64], in_=PS[k][:, 0:128].rearrange("p (g d) -> p g d", d=64), func=AF.Copy),
                 lambda e, k=k, kc=kc: e.tensor_copy(out=VS[:, kc, :, 0:64], in_=PS[k][:, 0:128].rearrange("p (g d) -> p g d", d=64)),
                 [("ps", k), "VSc"], [("VS", kc)])
            if "vm" in SUB or "v1" in SUB:
                pass
            else:
              evac(lambda e, k=k, kc=kc: e.activation(out=VW[:, kc % 8, :, 0:64], in_=PS[k][:, 128:256].rearrange("p (g d) -> p g d", d=64), func=AF.Copy),
                 lambda e, k=k, kc=kc: e.tensor_copy(out=VW[:, kc % 8, :, 0:64], in_=PS[k][:, 128:256].rearrange("p (g d) -> p g d", d=64)),
                 [("ps", k), "VWc"], [("VW", kc % 8)])
        if STOP < 2:
            S.add("sp", lambda e, t0=t0: e.dma_start(out=xTv[:, :, t0:t0 + 512], in_=xs), reads=XSK, writes=tkeys, dsem="mxwd")
            continue
        pc0 = 32 * G
        khs = [next_wb(), next_wb()]
        for kv in [int(x) for x in os.environ.get("KVS", "0,1").split(",")]:
            kh = khs[kv]
            rows = slice(0, 64) if kv == 0 else slice(64, 128)
            for g in range(int(os.environ.get("NGG", "2"))):
                for li in range(int(os.environ.get("NLI", "32"))):
                    rhs = KCT[rows, g, li * 33:li * 33 + 32] if li < 16 else KCT[rows, g, (li - 16) * 33 + 1:(li - 16) * 33 + 33]
                    pe(lambda e, kv=kv, g=g, li=li, rhs=rhs, rows=rows, kh=kh: e.matmul(PS[kh][:, g * 32:(g + 1) * 32], lhsT=W1[rows, li, :], rhs=rhs, start=(li == 0), stop=(li == int(os.environ.get("NLI", "32")) - 1)),
                       [("W1", li // 16), "KCT"], [("ps", kh)])
        CSUB = os.environ.get("CSUB", "abc")
        for kv in range(2 if "b" in CSUB else 0):
            Hd = HK if kv == 0 else HV
            hkey = "HK" if kv == 0 else "HV"
            gx = GX[:, kv * 2:kv * 2 + 2, :]
            gt = GT[:, kv * 2:kv * 2 + 2, :]
            src = PS[khs[kv]][:, 0:64].rearrange("p (g i) -> p g i", i=32)
            act(lambda e, kv=kv, gx=gx, src=src: e.activation(out=gx, in_=src, func=AF.Identity, bias=CB[:, kv:kv + 1]),
                [("ps", khs[kv]), "CB"], [("GX", kv)])
            dve(lambda e, gx=gx, gt=gt: e.tensor_tensor(out=gt, in0=gx, in1=gx, op=ALU.mult), [("GX", kv)], [("GT", kv)])
            dve(lambda e, gt=gt: e.tensor_scalar(out=gt, in0=gt, scalar1=0.044715, scalar2=1.0, op0=ALU.mult, op1=ALU.add), [("GT", kv)], [("GT", kv)])
            dve(lambda e, gx=gx, gt=gt: e.tensor_tensor(out=gt, in0=gt, in1=gx, op=ALU.mult), [("GT", kv), ("GX", kv)], [("GT", kv)])
            act(lambda e, gt=gt: e.activation(out=gt, in_=gt, func=AF.Sigmoid, scale=2.0 * 0.7978845608028654), [("GT", kv)], [("GT", kv)])
            dve(lambda e, gx=gx, gt=gt, Hd=Hd, pc0=pc0: e.tensor_tensor(out=Hd[:, :, pc0:pc0 + 32], in0=gx, in1=gt, op=ALU.mult),
                [("GT", kv), ("GX", kv)], [hkey])
        k = next_wb()
        for g in range(2 if "c" in CSUB else 0):
            pe(lambda e, g=g, k=k, pc0=pc0: e.matmul(PS[k][0:64, g * 32:(g + 1) * 32], lhsT=W2[:, 0, :], rhs=HK[:, g, pc0:pc0 + 32], start=True, stop=True),
               ["W2", "HK"], [("ps", k)])
        dve(lambda e, k=k, pc0=pc0: e.tensor_copy(out=KCC[0:64, :, pc0:pc0 + 32], in_=PS[k][0:64, 0:64].rearrange("p (g i) -> p g i", i=32)),
            [("ps", k), "KCC"], ["KCC"])
        ccn = G // 4
        k = next_wb()
        for g in range(2 if "c" in CSUB else 0):
            pe(lambda e, g=g, k=k, ccn=ccn: e.matmul(PS[k][:, g * 64:(g + 1) * 64], lhsT=HV[:, g, ccn * 128:(ccn + 1) * 128], rhs=W2[:, 1, :], start=True, stop=True),
               ["W2", "HV"], [("ps", k)])
        dve(lambda e, k=k, ccn=ccn: e.tensor_copy(out=VCA[:, ccn, :, 0:64], in_=PS[k][:, 0:128].rearrange("p (g d) -> p g d", d=64)),
            [("ps", k), "VCA"], ["VCA"])

        if STOP < 3:
            S.add("sp", lambda e, t0=t0: e.dma_start(out=xTv[:, :, t0:t0 + 512], in_=xs), reads=XSK, writes=tkeys, dsem="mxwd")
            continue
        ncc = G // 4 + 1

        def do_cmp(g):
            IAg = IA2[g]
            dve(lambda e: e.memset(IAg, 0.0), [], [("IA", g)])
            for hh in range(4):
                h = 4 * g + hh
                tiles = []
                for cc in range(ncc):
                    Dd = G - 4 * cc

                    def extra(pt, ptk, first, last, cc=cc):
                        for qt in range(4):
                            pe(lambda e, qt=qt, pt=pt, cc=cc: e.matmul(PS[5][:, qt * 65:(qt + 1) * 65], lhsT=pt[:, qt * 128:(qt + 1) * 128], rhs=OV[:, cc, :], start=(first and qt == 0), stop=last, skip_group_check=True),
                               [ptk, "OV"], [("ps", 5)])
                    tiles.append(dict(lhsT=KCC[0:65, g, cc * 128:(cc + 1) * 128], rhs=QA[0:65, h, :], rk=["KCC", ("QA", h, "q"), ("QA", h, "m")],
                                      mask=(MSK[:, 8 + Dd, :] if Dd <= 3 else None), bias=ABC[:, h, cc, G:G + 1], bk=["ABC"],
                                      v=VCA[:, cc, g, :], vk=["VCA"], extra=extra))
                cb = (2, 3, 4)[hh % 3]

                def after_cmp(h=h, hh=hh, cb=cb):
                    imv = PS[5][:, 0:260].rearrange("p (q n) -> p q n", n=65)
                    dve(lambda e, imv=imv: e.tensor_scalar(out=RL, in0=imv[:, :, 64], scalar1=1e-30, scalar2=None, op0=ALU.max), [("ps", 5)], ["RL"])
                    dve(lambda e: e.reciprocal(out=RL, in_=RL), ["RL"], ["RL"])
                    for qt in range(4):
                        dve(lambda e, qt=qt, imv=imv, IAg=IAg: e.scalar_tensor_tensor(out=IAg[:, qt, :], in0=imv[:, qt, 0:64], scalar=RL[:, qt:qt + 1], in1=IAg[:, qt, :], op0=ALU.mult, op1=ALU.add),
                            [("ps", 5), "RL", ("IA", g)], [("IA", g)])
                    gate_combine(h, 0, PS[cb], ("ps", cb), 0, 4 * g + hh)
                attn(tiles, PS[cb], ("ps", cb), done_cb=after_cmp)
            pipe_flush()

        def do_topk(g):
            IAg = IA2[g]
            SMg = SM2[g]
            dve(lambda e: e.tensor_tensor(out=IAg, in0=IAg, in1=FMg, op=ALU.add), [("IA", g), "FMg"], [("IA", g)])
            for qt in range(4):
                dve(lambda e, qt=qt: e.max(out=IMX[:, qt, :], in_=IAg[:, qt, :]), [("IA", g)], ["IMX"])
            for qt in range(4):
                dve(lambda e, qt=qt: e.tensor_scalar(out=SMg[:, qt, 64:128], in0=IAg[:, qt, :], scalar1=IMX[:, qt, 7:8], scalar2=1.0, op0=ALU.is_ge, op1=ALU.subtract),
                    [("IA", g), "IMX", ("SM", g)], [("SM", g)])
            for qt in range(4):
                pe(lambda e, qt=qt: e.matmul(PS[5][:, qt * 128:(qt + 1) * 128], lhsT=SMg[:, qt, :], rhs=c.identb, start=True, stop=True),
                   [("SM", g), "identb"], [("ps", 5)])
            for hh in range(4):
                h = 4 * g + hh
                dve(lambda e, h=h: e.tensor_copy(out=QA[64:128, h, :], in_=PS[5][64:128, :]),
                    [("ps", 5)], [("QA", h, "m")])
                dve(lambda e, h=h: e.tensor_scalar(out=QA[64:65, h, :], in0=JR[64:65, :], scalar1=-SLOPES[h], scalar2=None, op0=ALU.mult),
                    ["JR", ("QA", h, "m")], [("QA", h, "m")])

        def do_head(g, hh):
            h = 4 * g + hh
            tiles = []
            for kc in range(4 * G + 4):
                r = kc - 4 * G
                tiles.append(dict(lhsT=KS[:, g, kc * 128:(kc + 1) * 128], rhs=QA[:, h, :], rk=[("KSc", g), ("KS", g, kc // 4), ("QA", h, "q"), ("QA", h, "m")],
                                  mask=(MSK[:, r, :] if r >= 0 else None), bias=AB[:, h, r + NT - 4:r + NT - 3], bk=["AB"],
                                  v=VS[:, kc, g, :], vk=[("VS", kc), "VSc"]))
            attn(tiles, PS[3], ("ps", 3), banks=(0, 1, 7, 2), done_cb=lambda h=h, g=g, hh=hh: gate_combine(h, 1, PS[3], ("ps", 3), 1, 4 * g + hh))
            tiles = []
            for kc in range(max(0, 4 * G - 4), 4 * G + 4):
                r = kc - 4 * G
                mi = r if r >= 0 else 8 + r
                sl = kc % 8
                tiles.append(dict(lhsT=KW[:, g, sl * 128:(sl + 1) * 128], rhs=QA[:, h, :], rk=["KWc", ("KW", g, (kc // 4) % 2), ("QA", h, "q"), ("QA", h, "m")],
                                  mask=MSK[:, mi, :], bias=AB[:, h, r + NT - 4:r + NT - 3], bk=["AB"],
                                  v=VW[:, sl, g, :], vk=[("VW", sl), "VWc"]))
            attn(tiles, PS[4], ("ps", 4), banks=(0, 1, 7, 2), done_cb=lambda h=h, g=g, hh=hh: gate_combine(h, 2, PS[4], ("ps", 4), 2, 4 * g + hh))

        do_cmp(0)
        do_cmp(1)
        do_topk(0)
        do_head(0, 0)
        do_topk(1)
        for hh in range(1, 4):
            do_head(0, hh)
        for hh in range(4):
            do_head(1, hh)
        pipe_flush()
        if STOP < 6:
            S.add("sp", lambda e, t0=t0: e.dma_start(out=xTv[:, :, t0:t0 + 512], in_=xs), reads=XSK, writes=tkeys, dsem="mxwd")
            continue
        srcs = [None] * 4
        for (ja, jb) in ((0, 2), (2, 4)):
            cur, base = U, 0
            for kk in range(jb):
                sh = 1 << kk
                lo = max(kk, ja)
                dst = UW[kk % 2]
                dve(lambda e, cur=cur, base=base, dst=dst, sh=sh, lo=lo, ja=ja, jb=jb: e.tensor_tensor(out=dst[:, lo - ja:jb - ja, sh:528], in0=cur[:, lo - base:jb - base, sh:528], in1=cur[:, lo - base:jb - base, 0:528 - sh], op=ALU.add),
                    ["U", "UW0", "UW1"], ["UW%d" % (kk % 2)])
                if ja <= kk < jb:
                    srcs[kk] = dst[:, kk - ja, :]
                cur, base = dst, ja
            for j in range(ja, jb):
                w = POOL_W[j]
                src = srcs[j]
                if G == 0:
                    dve(lambda e, src=src, j=j: e.tensor_tensor(out=src[:, 16:32], in0=src[:, 16:32], in1=IC0[:, j, :], op=ALU.mult),
                        ["UW0", "UW1", "IC0"], ["UW%d" % (j % 2)])
                dve(lambda e, src=src, j=j, w=w: e.scalar_tensor_tensor(out=Y2[j % 2], in0=src[:, 16:528], scalar=1.0 / w, in1=U[:, j, 16:528], op0=ALU.mult, op1=ALU.subtract),
                    ["UW0", "UW1", "U"], [("Y", j % 2)])
                k = next_wb()
                pe(lambda e, j=j, k=k: e.matmul(PS[k], lhsT=PW[:, j, :], rhs=Y2[j % 2], start=True, stop=True), ["PW", ("Y", j % 2)], [("ps", k)])
                dve(lambda e, j=j, k=k: e.tensor_scalar(out=OPt[:, j, :], in0=PS[k], scalar1=PSC[:, j:j + 1], scalar2=None, op0=ALU.mult),
                    [("ps", k), "PSC"], [("OP", j)])
        for o in range(NKC):
            k = next_wb()
            for j in range(8):
                rhs = ON[:, j, :] if j < 4 else OPt[:, j - 4, :]
                rk = [("ON", 2 * j), ("ON", 2 * j + 1)] if j < 4 else [("OP", j - 4)]
                pe(lambda e, o=o, j=j, k=k, rhs=rhs: e.matmul(PS[k], lhsT=WO[:, j, o * 128:(o + 1) * 128], rhs=rhs, start=(j == 0), stop=(j == 7)),
                   [("WO", j)] + rk, [("ps", k)])
            dve(lambda e, o=o, k=k: e.tensor_tensor(out=xs[:, o, :], in0=PS[k], in1=xs[:, o, :], op=ALU.add),
                [("ps", k), ("xs", o)], [("xs", o)])
            S.add("sp", lambda e, t0=t0, o=o: e.dma_start(out=xTv[:, o, t0:t0 + 512], in_=xs[:, o, :]),
                  reads=[("xs", o)], writes=[("xTc", G, o)], dsem=("mxw", o))
        S.add("sp", lambda e: e.nop(), reads=[("xTc", G, o) for o in range(NKC)], writes=tkeys)


def host_pack(inp, L, with_mixer=True):
    f32 = np.float32
    wgu = np.empty((L, 2, NF, 128, 2, NKC, 128), f32)
    wd = np.empty((L, 2, NKC, 128, NF, 128), f32)
    ffn_w = ((inp["ffn1_wg"], inp["ffn1_wu"], inp["ffn1_wd"]), (inp["ffn2_wg"], inp["ffn2_wu"], inp["ffn2_wd"]))
    for l in range(L):
        for w in range(2):
            for j in range(2):
                a = np.asarray(ffn_w[w][j][l], f32).reshape(NKC, 128, NF, 128)
                wgu[l, w, :, :, j] = a.transpose(2, 1, 0, 3)
            a = np.asarray(ffn_w[w][2][l], f32).reshape(NF, 128, NKC, 128)
            wd[l, w] = a.transpose(2, 1, 0, 3)
    gam = np.empty((128, L * 24 + 8), f32)
    for l in range(L):
        for j, nm in enumerate(("ffn1_norm", "mix_norm", "ffn2_norm")):
            gam[:, l * 24 + j * 8:l * 24 + j * 8 + 8] = np.asarray(inp[nm][l], f32).reshape(8, 128).T
    gam[:, L * 24:] = np.asarray(inp["final_norm"], f32).reshape(8, 128).T
    T = int(np.asarray(inp["x"]).shape[1])
    NT, NG = T // 128, T // 512
    NCC = (NG + 3) // 4
    g = lambda k: np.asarray(inp[k], f32)
    perm = np.concatenate([np.arange(0, 512), np.arange(512, 576), np.arange(640, 704), np.arange(576, 640), np.arange(704, 768),
                           np.arange(768, 1024), np.arange(1152, 1280), np.arange(1024, 1152), np.arange(1280, INW)])
    win = g("w_in")[:, :, perm].reshape(L, NKC, 128, INW).transpose(0, 2, 1, 3).reshape(L, 128, NKC * INW)
    wout = g("w_out").reshape(L, NKC, 128, D).transpose(0, 2, 1, 3).reshape(L, 128, NKC * D)
    w1 = np.concatenate([g("cmp_wk1").reshape(L, 32, 64, 128).transpose(0, 2, 1, 3),
                         g("cmp_wv1").reshape(L, 32, 64, 128).transpose(0, 2, 1, 3)], axis=1).reshape(L, 128, 32 * 128)
    w2 = np.stack([g("cmp_wk2"), g("cmp_wv2")], axis=2).reshape(L, 128, 128)
    pet = np.concatenate([g("cmp_pe_k").transpose(0, 2, 1), g("cmp_pe_v").transpose(0, 2, 1)], axis=1)
    pw = g("pool_w").transpose(0, 2, 1, 3).reshape(L, 128, 512)
    psc = g("pool_scale").reshape(L, 4, 128).transpose(0, 2, 1)
    p = np.arange(128)[:, None]
    j = np.arange(512)[None, :]
    ckrow = np.zeros((64, T), f32)
    ckrow[0, :] = 1.0
    key = np.arange(T)
    for r in range(1, 64):
        ckrow[r, key // 64 == r] = BIG
    masks = np.zeros((12, 128, 512), f32)
    for r in range(4):
        masks[r] = np.where(128 * r + p <= j, 0.0, -BIG)
        masks[4 + r] = np.where(j < 512 + 128 * (r - 4) + p, 0.0, -BIG)
        masks[8 + r] = np.where(16 * p + 15 - 512 * r <= j, 0.0, -BIG)
    sl = np.asarray(SLOPES, f32)
    cab = np.zeros((128, 8, NT), f32)
    for i in range(NT):
        rel = i - (NT - 4)
        cab[:, :, i] = sl[None, :] * (128.0 * rel + p - 256.0)
    cabc = np.zeros((128, 8, NCC, NG), f32)
    for cc in range(NCC):
        for G in range(NG):
            cabc[:, :, cc, G] = sl[None, :] * (2048.0 * cc + 16.0 * p + 15.0 - 512.0 * G - 256.0)
    cabc[0, :, 0, :] = -1e30
    cjr = (np.arange(512, dtype=f32) - 256.0)[None, :]
    cfm = np.zeros((128, NT, 64), f32)
    n = np.arange(64)[None, :]
    for a in range(NT):
        cur = (128 * a + p) // 64
        cfm[:, a, :] = np.where((n == 0) | (n == cur) | (n == cur - 1), 8.0, 0.0)
    cov = np.zeros((128, NCC, 65), f32)
    for cc in range(NCC):
        pc = 128 * cc + p
        cblk = pc - 1
        ov = (16 * cblk <= 64 * n + 63) & (16 * cblk + 31 >= 64 * n) & (pc >= 1)
        cov[:, cc, 0:64] = ov
        cov[:, cc, 64] = 1.0
    csel = np.zeros((24, 24, 64), f32)
    for k in range(24):
        csel[k, k, :] = 1.0
    cic0 = np.zeros((128, 4, 16), f32)
    for jj, w in enumerate(POOL_W):
        t = np.arange(16)
        cic0[:, jj, :] = (w / np.minimum(t + 1, w))[None, :]
    m = {
        "win": np.ascontiguousarray(win), "wout": np.ascontiguousarray(wout), "w1": np.ascontiguousarray(w1),
        "w2": np.ascontiguousarray(w2), "pet": np.ascontiguousarray(pet), "pw": np.ascontiguousarray(pw),
        "psc": np.ascontiguousarray(psc), "ckrow": ckrow, "cmask": masks,
        "cab": cab.reshape(128, 8 * NT), "cabc": cabc.reshape(128, 8 * NCC * NG), "cjr": cjr,
        "cfm": cfm.reshape(128, NT * 64), "cov": cov.reshape(128, NCC * 65), "csel": csel.reshape(24, 24 * 64),
        "cic0": cic0.reshape(128, 64),
        "wgu": wgu.reshape(L, 2, NF, 128, 2 * NKC * 128),
        "wd": wd.reshape(L, 2, NKC, 128, NF * 128),
        "gam": gam,
        "cident": np.eye(128, dtype=f32),
    }
    if not with_mixer:
        for k in ("win", "wout", "w1", "w2", "pet", "pw", "psc", "ckrow", "cmask", "cab", "cabc", "cjr", "cfm", "cov", "csel", "cic0"):
            m.pop(k)
    return m


_CACHE = {}
WITH_MIXER = True


def kernel(**inputs):
    x = np.asarray(inputs["x"], np.float32)
    B, T, _ = x.shape
    L = int(np.asarray(inputs["ffn1_wg"]).shape[0])
    key = (T, L)
    if key not in _CACHE:
        _CACHE[key] = build(T, L, WITH_MIXER)
    nc = _CACHE[key]
    shared = host_pack(inputs, L, WITH_MIXER)
    in_maps = []
    for b in range(B):
        m = dict(shared)
        m["x"] = np.ascontiguousarray(x[b])
        in_maps.append(m)
    res = run_bass_kernel_spmd(nc, in_maps, core_ids=list(range(B)))
    return np.stack([np.asarray(r["out"], np.float32) for r in res.results], axis=0)
```

```python
import contextlib
import os
import numpy as np
import ml_dtypes
import concourse.bass as bass
import concourse.mybir as mybir
from concourse.bass_utils import run_bass_kernel_spmd

F32 = mybir.dt.float32
BF16 = mybir.dt.bfloat16
AF = mybir.ActivationFunctionType
ALU = mybir.AluOpType

D = 1024
DFF = 2816
NF = 22
NKC = 8
H = 8
DH = 64
INW = 1816
EPS = 1e-6
BIG = 30000.0
SLOPES = [2.0 ** (-(h + 1)) for h in range(8)]
POOL_W = (2, 4, 8, 16)


class _Op:
    __slots__ = ("eng", "emit", "reads", "writes", "dsem", "deps", "sig", "waits", "signaled")

    def __init__(self, eng, emit, reads, writes, dsem):
        self.eng = eng
        self.emit = emit
        self.reads = reads
        self.writes = writes
        self.dsem = dsem
        self.deps = ()
        self.sig = None
        self.waits = ()
        self.signaled = False


class Sched:
    ENGS = ("pe", "act", "dve", "pool", "sp")
    ROT = 20000

    def __init__(self, nc):
        self.nc = nc
        self.ops = []

    def add(self, eng, emit, reads=(), writes=(), dsem=None):
        self.ops.append(_Op(eng, emit, tuple(reads) + ("PHASE",), tuple(writes), dsem))

    def barrier(self):
        op = _Op("sp", lambda e: e.nop(), (), ("PHASE",), None)
        op.signaled = "barrier"
        self.ops.append(op)

    def analyze(self):
        last_w = {}
        readers = {}
        last_eng = {}
        last_dsem = {}
        for i, op in enumerate(self.ops):
            if op.signaled == "barrier":
                op.deps = set(last_eng.values()) | set(last_dsem.values())
                op.signaled = False
                last_w["PHASE"] = i
                readers["PHASE"] = {}
                last_eng[op.eng] = i
                continue
            if op.dsem is not None:
                last_dsem[op.dsem] = i
            else:
                last_eng[op.eng] = i
            deps = set()
            for r in op.reads:
                w = last_w.get(r)
                if w is not None:
                    deps.add(w)
            for k in op.writes:
                w = last_w.get(k)
                if w is not None:
                    deps.add(w)
                deps.update(readers.get(k, {}).values())
            deps.discard(i)
            rk = ("d", op.dsem) if op.dsem is not None else op.eng
            for r in op.reads:
                readers.setdefault(r, {})[rk] = i
            for k in op.writes:
                last_w[k] = i
                readers[k] = {}
            op.deps = deps
        for op in self.ops:
            for d in op.deps:
                dop = self.ops[d]
                if dop.dsem is None and dop.eng == "pe" and op.eng == "pe" and op.dsem is None:
                    continue
                dop.signaled = True
        eng_cnt = {e: 0 for e in self.ENGS}
        eng_gen = {e: 0 for e in self.ENGS}
        dsem_cnt = {}
        self.sem_names = set()
        known = {e: {} for e in self.ENGS}
        for i, op in enumerate(self.ops):
            waits = {}
            for d in op.deps:
                dop = self.ops[d]
                if dop.dsem is not None:
                    s = ("d", dop.dsem)
                    v = dsem_cnt[dop.dsem]
                else:
                    if dop.eng == "pe" and op.eng == "pe" and op.dsem is None:
                        continue
                    s, v = dop.sig
                if known[op.eng].get(s, 0) >= v:
                    continue
                if waits.get(s, 0) < v:
                    waits[s] = v
            for s, v in waits.items():
                known[op.eng][s] = v
            op.waits = tuple(waits.items())
            if op.dsem is not None:
                dsem_cnt[op.dsem] = dsem_cnt.get(op.dsem, 0) + 16
                op.sig = (("d", op.dsem), dsem_cnt[op.dsem])
                self.sem_names.add(("d", op.dsem))
            elif op.signaled:
                e = op.eng
                if eng_cnt[e] >= self.ROT:
                    eng_gen[e] += 1
                    eng_cnt[e] = 0
                eng_cnt[e] += 1
                s = ("e", e, eng_gen[e])
                op.sig = (s, eng_cnt[e])
                self.sem_names.add(s)
        self.dsem_tot = dsem_cnt

    def emit(self, final_waits=()):
        nc = self.nc
        self.analyze()
        sems = {}
        with contextlib.ExitStack() as es:
            for n, s in enumerate(sorted(self.sem_names, key=str)):
                sems[s] = es.enter_context(nc.semaphore("sm%d" % n))
            block = es.enter_context(nc.Block())
            by_eng = {e: [op for op in self.ops if op.eng == e] for e in self.ENGS}

            def run(engobj, ops):
                for op in ops:
                    for s, v in op.waits:
                        engobj.wait_ge(sems[s], v)
                    ins = op.emit(engobj)
                    if op.dsem is not None:
                        ins.then_inc(sems[op.sig[0]], 16)
                    elif op.signaled:
                        ins.then_inc(sems[op.sig[0]], 1)

            @block.tensor
            def _(e):
                run(e, by_eng["pe"])

            @block.scalar
            def _(e):
                run(e, by_eng["act"])

            @block.vector
            def _(e):
                run(e, by_eng["dve"])

            @block.gpsimd
            def _(e):
                run(e, by_eng["pool"])

            @block.sync
            def _(e):
                run(e, by_eng["sp"])
                for key in final_waits:
                    e.wait_ge(sems[("d", key)], self.dsem_tot[key])


class Arena:
    def __init__(self, nc, nbytes):
        self.t = nc.alloc_sbuf_tensor("arena", [128, nbytes // 4], F32)
        self.cap = nbytes
        self.off = 0
        self.n = 0

    def alloc(self, shape, dtype):
        esz = 2 if dtype == BF16 else 4
        n = int(np.prod(shape))
        nb = (n * esz + 63) // 64 * 64
        if self.off + nb > self.cap:
            raise RuntimeError("arena overflow: need %d have %d" % (self.off + nb, self.cap))
        v = self.t[:, self.off // 4:(self.off + nb) // 4]
        self.off += nb
        if dtype == BF16:
            v = v.bitcast(BF16)
        v = v[:, 0:n]
        if len(shape) == 2:
            v = v.rearrange("p (a b) -> p a b", b=shape[1])
        elif len(shape) == 3:
            v = v.rearrange("p (a b c) -> p a b c", b=shape[1], c=shape[2])
        return v


class Ctx:
    pass


def build(T, L, with_mixer=True):
    nc = bass.Bass("TRN2", target_bir_lowering=False)
    S = Sched(nc)
    c = Ctx()
    c.nc, c.S, c.T, c.L = nc, S, T, L
    NG = T // 512
    c.NG = NG

    def din(name, shape, dt=F32):
        return nc.dram_tensor(name, list(shape), dt, kind="ExternalInput").ap()

    c.x_in = din("x", [T, D])
    c.wgu = din("wgu", [L, 2, NF, 128, 2 * NKC * 128])
    c.wd = din("wd", [L, 2, NKC, 128, NF * 128])
    c.gam = din("gam", [128, L * 24 + 8])
    c.cident = din("cident", [128, 128])
    if with_mixer:
        declare_mixer_inputs(c, din)
    c.out = nc.dram_tensor("out", [T, D], F32, kind="ExternalOutput").ap()
    c.xT = nc.dram_tensor("xT", [D, T], F32, kind="Internal").ap()

    c.A = Arena(nc, 207 * 1024)
    c.PS = [nc.alloc_psum_tensor("psb%d" % i, [128, 512], F32)[:] for i in range(8)]
    A = c.A
    c.ident32 = A.alloc((128,), F32)
    c.identb = A.alloc((128,), BF16)
    c.onesb = A.alloc((128,), BF16)
    c.gamt = A.alloc((L * 24 + 8,), F32)
    S.add("sp", lambda e: e.dma_start(out=c.ident32, in_=c.cident), writes=["ident32"], dsem="c0")
    S.add("sp", lambda e: e.dma_start(out=c.gamt, in_=c.gam), writes=["gam"], dsem="c1")
    S.add("dve", lambda e: e.tensor_copy(out=c.identb, in_=c.ident32), reads=["ident32"], writes=["identb"])
    S.add("dve", lambda e: e.memset(c.onesb, 1.0), writes=["onesb"])
    c.tinyt = A.alloc((1,), F32)
    S.add("dve", lambda e: e.memset(c.tinyt, 1e-30), writes=["tinyt"])
    c.epst = A.alloc((1,), F32)
    S.add("dve", lambda e: e.memset(c.epst, EPS), writes=["epst"])
    c.base = A.off

    phase_in(c)
    for l in range(L):
        phase_ffn(c, l, 0)
        if with_mixer:
            phase_mixer(c, l)
        phase_ffn(c, l, 1)
    phase_out(c)
    S.emit(final_waits=["out"])
    return nc


def norm_stats(c, xs, cols, sq, stat_ps, rstd, kx, tag, pskey=None, kxs=None):
    S = c.S
    if pskey is None:
        pskey = ("ps", tag)
    for ch in range(NKC):
        sqt = sq[ch % 2]
        S.add("act", lambda e, sqt=sqt, ch=ch: e.activation(out=sqt, in_=xs[:, ch, cols], func=AF.Square),
              reads=[kx if kxs is None else kxs[ch]], writes=[("sq", tag, ch % 2)])
        S.add("pe", lambda e, sqt=sqt, ch=ch: e.matmul(stat_ps, lhsT=c.onesb, rhs=sqt, start=(ch == 0), stop=(ch == NKC - 1)),
              reads=[("sq", tag, ch % 2), "onesb"], writes=[pskey])
    S.add("act", lambda e: e.activation(out=rstd, in_=stat_ps, func=AF.Ln, scale=1.0 / D, bias=c.epst),
          reads=[pskey, "epst"], writes=[("rstd", tag)])
    S.add("act", lambda e: e.activation(out=rstd, in_=rstd, func=AF.Exp, scale=-0.5),
          reads=[("rstd", tag)], writes=[("rstd", tag)])


def phase_in(c):
    S, A, nc = c.S, c.A, c.nc
    S.barrier()
    A.off = c.base
    NT = c.T // 128
    xin = [A.alloc((D,), F32) for _ in range(2)]
    xo = [A.alloc((NKC, 128), F32) for _ in range(2)]
    for tt in range(NT):
        b = tt % 2
        S.add("sp", lambda e, tt=tt, b=b: e.dma_start(out=xin[b], in_=c.x_in[tt * 128:(tt + 1) * 128, :]),
              writes=[("xin", b)], dsem=("xin", b))
        for half in range(2):
            ps = c.PS[(tt * 2 + half) % 4]
            for q in range(4):
                ch = half * 4 + q
                S.add("pe", lambda e, ps=ps, q=q, ch=ch, b=b: e.transpose(out=ps[:, q * 128:(q + 1) * 128], in_=xin[b][:, ch * 128:(ch + 1) * 128], identity=c.ident32),
                      reads=[("xin", b), "ident32"], writes=[("ps", (tt * 2 + half) % 4)])
            eng = "act" if half == 0 else "dve"
            dst = xo[b][:, half * 4:(half + 1) * 4, :]
            src = ps.rearrange("p (a b) -> p a b", b=128)
            if eng == "act":
                S.add("act", lambda e, dst=dst, src=src: e.activation(out=dst, in_=src, func=AF.Copy),
                      reads=[("ps", (tt * 2 + half) % 4)], writes=[("xo", b, half)])
            else:
                S.add("dve", lambda e, dst=dst, src=src: e.tensor_copy(out=dst, in_=src),
                      reads=[("ps", (tt * 2 + half) % 4)], writes=[("xo", b, half)])
        S.add("sp", lambda e, tt=tt, b=b: e.dma_start(out=c.xT.rearrange("(c p) t -> p c t", p=128)[:, :, tt * 128:(tt + 1) * 128], in_=xo[b]),
              reads=[("xo", b, 0), ("xo", b, 1)], writes=[("xT", tt)], dsem=("xo", b))


def phase_out(c):
    S, A, nc = c.S, c.A, c.nc
    S.barrier()
    A.off = c.base
    NG = c.NG
    gcol = c.L * 24
    xs = [A.alloc((NKC, 512), F32) for _ in range(2)]
    xn = A.alloc((NKC, 512), F32)
    sq = [A.alloc((512,), BF16) for _ in range(2)]
    rstd = A.alloc((512,), F32)
    yo = [A.alloc((D,), F32) for _ in range(2)]
    xTv = c.xT.rearrange("(c p) t -> p c t", p=128)
    for g in range(NG):
        b = g % 2
        S.add("sp", lambda e, g=g, b=b: e.dma_start(out=xs[b], in_=xTv[:, :, g * 512:(g + 1) * 512]),
              reads=[("xT", g * 4 + i) for i in range(4)], writes=[("xs", b)], dsem=("xs", b))
        norm_stats(c, xs[b], slice(0, 512), sq, c.PS[7], rstd, ("xs", b), "fo")
        for ch in range(NKC):
            S.add("dve", lambda e, ch=ch, b=b: e.scalar_tensor_tensor(out=xn[:, ch, :], in0=xs[b][:, ch, :], scalar=c.gamt[:, gcol + ch:gcol + ch + 1], in1=rstd, op0=ALU.mult, op1=ALU.mult),
                  reads=[("xs", b), ("rstd", "fo"), "gam"], writes=[("xno", ch)])
        for tt in range(4):
            ob = (g * 4 + tt) % 2
            for half in range(2):
                pk = (g * 8 + tt * 2 + half) % 4
                ps = c.PS[pk]
                for q in range(4):
                    ch = half * 4 + q
                    S.add("pe", lambda e, ps=ps, q=q, ch=ch, tt=tt: e.transpose(out=ps[:, q * 128:(q + 1) * 128], in_=xn[:, ch, tt * 128:(tt + 1) * 128], identity=c.ident32),
                          reads=[("xno", ch), "ident32"], writes=[("ps", pk)])
                dst = yo[ob][:, half * 512:(half + 1) * 512]
                if half == 0:
                    S.add("act", lambda e, dst=dst, ps=ps: e.activation(out=dst, in_=ps, func=AF.Copy),
                          reads=[("ps", pk)], writes=[("yo", ob, half)])
                else:
                    S.add("dve", lambda e, dst=dst, ps=ps: e.tensor_copy(out=dst, in_=ps),
                          reads=[("ps", pk)], writes=[("yo", ob, half)])
            t0 = g * 512 + tt * 128
            S.add("sp", lambda e, t0=t0, ob=ob: e.dma_start(out=c.out[t0:t0 + 128, :], in_=yo[ob]),
                  reads=[("yo", ob, 0), ("yo", ob, 1)], dsem="out")


def phase_ffn(c, l, which):
    S, A, nc, T = c.S, c.A, c.nc, c.T
    S.barrier()
    A.off = c.base
    NP = 1024 if T >= 1024 else 512
    NS = NP // 512
    npass = T // NP
    gcol = l * 24 + (0 if which == 0 else 16)
    tagp = ("f", l, which)
    xsb = [A.alloc((NKC, NP), F32) for _ in range(2)]
    xn = A.alloc((NKC, NP), BF16)
    hT = A.alloc((NF, NP), BF16)
    sq = [A.alloc((512,), BF16) for _ in range(2)]
    rstd = A.alloc((512,), F32)
    sg = [A.alloc((512,), F32) for _ in range(2)]
    RGU, RD = 6, 4
    gu_slots = [A.alloc((2, NKC, 128), BF16) for _ in range(RGU)]
    d_slots = [A.alloc((NF, 128), BF16) for _ in range(RD)]
    xTv = c.xT.rearrange("(c p) t -> p c t", p=128)

    gu_loads = [(ps_, f) for ps_ in range(npass) for f in range(NF)]
    d_loads = [(ps_, o) for ps_ in range(npass) for o in range(NKC)]
    st = {"gu": 0, "d": 0}

    def issue_gu(upto):
        while st["gu"] < min(upto, len(gu_loads)):
            i = st["gu"]
            _, f = gu_loads[i]
            slot = i % RGU
            S.add("pool", lambda e, f=f, slot=slot: e.dma_start(out=gu_slots[slot], in_=c.wgu[l, which, f].rearrange("p (a b c) -> p a b c", a=2, b=NKC)),
                  writes=[("gu", slot)], dsem=("gu", slot))
            st["gu"] += 1

    def issue_d(upto):
        while st["d"] < min(upto, len(d_loads)):
            i = st["d"]
            _, o = d_loads[i]
            slot = i % RD
            S.add("pool", lambda e, o=o, slot=slot: e.dma_start(out=d_slots[slot], in_=c.wd[l, which, o].rearrange("p (a b) -> p a b", a=NF)),
                  writes=[("wd", slot)], dsem=("wd", slot))
            st["d"] += 1

    issue_gu(RGU)
    issue_d(RD - 1)
    gi = 0
    di = 0
    def load_x(p_):
        t0 = p_ * NP
        xb = p_ % 2
        S.add("sp", lambda e, t0=t0, xs=xsb[xb]: e.dma_start(out=xs, in_=xTv[:, :, t0:t0 + NP]),
              reads=[("xT", i) for i in range(t0 // 128, (t0 + NP) // 128)], writes=[("xs", xb)], dsem=("xs", xb))

    def do_norm(p_, s):
        xb = p_ % 2
        xs = xsb[xb]
        kxs = ("xs", xb)
        cols = slice(s * 512, (s + 1) * 512)
        norm_stats(c, xs, cols, sq, c.PS[6], rstd, kxs, "ff")
        for ch in range(NKC):
            S.add("dve", lambda e, ch=ch, cols=cols, xs=xs: e.scalar_tensor_tensor(out=xn[:, ch, cols], in0=xs[:, ch, cols], scalar=c.gamt[:, gcol + ch:gcol + ch + 1], in1=rstd, op0=ALU.mult, op1=ALU.mult),
                  reads=[kxs, ("rstd", "ff"), "gam"], writes=[("xn", s)])

    load_x(0)
    for s in range(NS):
        do_norm(0, s)
    for p_ in range(npass):
        t0 = p_ * NP
        xb = p_ % 2
        xs = xsb[xb]
        kxs = ("xs", xb)
        tkeys = [("xT", i) for i in range(t0 // 128, (t0 + NP) // 128)]
        if p_ + 1 < npass:
            load_x(p_ + 1)
        for f in range(NF):
            slot = gi % RGU
            for s in range(NS):
                cols = slice(s * 512, (s + 1) * 512)
                k = (f * NS + s) % 2
                pg, pu = c.PS[k], c.PS[2 + k]
                for ch in range(NKC):
                    S.add("pe", lambda e, pg=pg, ch=ch, slot=slot, cols=cols: e.matmul(pg, lhsT=gu_slots[slot][:, 0, ch, :], rhs=xn[:, ch, cols], start=(ch == 0), stop=(ch == NKC - 1)),
                          reads=[("gu", slot), ("xn", s)], writes=[("ps", k)])
                for ch in range(NKC):
                    S.add("pe", lambda e, pu=pu, ch=ch, slot=slot, cols=cols: e.matmul(pu, lhsT=gu_slots[slot][:, 1, ch, :], rhs=xn[:, ch, cols], start=(ch == 0), stop=(ch == NKC - 1)),
                          reads=[("gu", slot), ("xn", s)], writes=[("ps", 2 + k)])
                S.add("act", lambda e, pg=pg, k=k: e.activation(out=sg[k], in_=pg, func=AF.Silu),
                      reads=[("ps", k)], writes=[("sg", k)])
                S.add("dve", lambda e, pu=pu, k=k, f=f, cols=cols: e.tensor_tensor(out=hT[:, f, cols], in0=pu, in1=sg[k], op=ALU.mult),
                      reads=[("ps", 2 + k), ("sg", k)], writes=[("hT", f, s)])
            gi += 1
            issue_gu(gi + RGU)
        for o in range(NKC):
            slot = di % RD
            for s in range(NS):
                cols = slice(s * 512, (s + 1) * 512)
                k = 4 + (o * NS + s) % 2
                py = c.PS[k]
                for f in range(NF):
                    S.add("pe", lambda e, py=py, f=f, slot=slot, cols=cols: e.matmul(py, lhsT=d_slots[slot][:, f, :], rhs=hT[:, f, cols], start=(f == 0), stop=(f == NF - 1)),
                          reads=[("wd", slot), ("hT", f, s)], writes=[("ps", k)])
                S.add("dve", lambda e, py=py, o=o, cols=cols, xs=xs: e.scalar_tensor_tensor(out=xs[:, o, cols], in0=py, scalar=0.5, in1=xs[:, o, cols], op0=ALU.mult, op1=ALU.add),
                      reads=[("ps", k), kxs], writes=[("xso", xb, o)])
            di += 1
            issue_d(di + RD - 1)
            if p_ + 1 < npass and 1 <= o <= NS:
                do_norm(p_ + 1, o - 1)
        S.add("sp", lambda e, t0=t0, xs=xs: e.dma_start(out=xTv[:, :, t0:t0 + NP], in_=xs),
              reads=[kxs] + [("xso", xb, o) for o in range(NKC)], writes=tkeys, dsem="xsw")


def declare_mixer_inputs(c, din):
    L, T = c.L, c.T
    NT, NG = T // 128, T // 512
    NCC = (NG + 3) // 4
    c.NCC = NCC
    c.win = din("win", [L, 128, NKC * INW])
    c.wout = din("wout", [L, 128, NKC * D])
    c.w1 = din("w1", [L, 128, 32 * 128])
    c.w2 = din("w2", [L, 128, 128])
    c.pet = din("pet", [L, 128, 32])
    c.pw = din("pw", [L, 128, 512])
    c.psc = din("psc", [L, 128, 4])
    c.ckrow = din("ckrow", [64, T])
    c.cmask = din("cmask", [12, 128, 512])
    c.cab = din("cab", [128, 8 * NT])
    c.cabc = din("cabc", [128, 8 * NCC * NG])
    c.cjr = din("cjr", [1, 512])
    c.cfm = din("cfm", [128, NT * 64])
    c.cov = din("cov", [128, NCC * 65])
    c.csel = din("csel", [24, 24 * 64])
    c.cic0 = din("cic0", [128, 64])


def phase_mixer(c, l):
    S, A, nc, T = c.S, c.A, c.nc, c.T
    S.barrier()
    A.off = c.base
    NT, NG, NCC = T // 128, T // 512, c.NCC
    gcol = l * 24 + 8
    PS = c.PS
    WIN = A.alloc((NKC, INW), BF16)
    WO = A.alloc((NKC, D), BF16)
    W1 = A.alloc((32, 128), BF16)
    W2 = A.alloc((2, 64), BF16)
    PET = A.alloc((32,), BF16)
    PW = A.alloc((4, 128), BF16)
    PSC = A.alloc((4,), F32)
    CB = A.alloc((2,), F32)
    KS = A.alloc((2, T), BF16)
    KW = A.alloc((2, 1024), BF16)
    KCC = A.alloc((2, 128 * NCC), BF16)
    VS = A.alloc((NT, 2, 128), BF16)
    VW = A.alloc((8, 2, 128), BF16)
    VCA = A.alloc((NCC, 2, 128), BF16)
    HK = A.alloc((2, 128 * NCC), BF16)
    HV = A.alloc((2, 128 * NCC), BF16)
    KCT = A.alloc((2, 528), BF16)
    xs = A.alloc((NKC, 512), F32)
    xn = A.alloc((NKC, 512), BF16)
    sq = [A.alloc((512,), BF16) for _ in range(2)]
    rstd = A.alloc((512,), F32)
    QA = A.alloc((8, 512), BF16)
    ON = A.alloc((4, 512), BF16)
    OPt = A.alloc((4, 512), BF16)
    NPT = 4
    PT = [A.alloc((512,), BF16) for _ in range(NPT)]
    U = A.alloc((4, 528), F32)
    UW = [A.alloc((2, 528), F32), A.alloc((2, 528), F32)]
    Y2 = [A.alloc((512,), BF16) for _ in range(2)]
    SG = A.alloc((512,), BF16)
    IA2 = [A.alloc((4, 64), F32) for _ in range(2)]
    IMX = A.alloc((4, 8), F32)
    RL = A.alloc((4,), F32)
    SM2 = [A.alloc((4, 128), BF16) for _ in range(2)]
    LM = A.alloc((512,), F32)
    GR = A.alloc((512,), F32)
    TMP = LM
    ACC = A.alloc((4, 512), F32)
    GX = A.alloc((4, 32), F32)
    GT = A.alloc((4, 32), F32)
    MSK = A.alloc((12, 512), BF16)
    AB = A.alloc((8, NT), F32)
    ABC = A.alloc((8, NCC, NG), F32)
    JR = A.alloc((512,), F32)
    FMg = A.alloc((4, 64), F32)
    OV = A.alloc((NCC, 65), BF16)
    SEL = A.alloc((24, 64), BF16)
    IC0 = A.alloc((4, 16), F32)

    def dve(fn, reads, writes):
        S.add("dve", fn, reads=reads, writes=writes)

    def act(fn, reads, writes):
        S.add("act", fn, reads=reads, writes=writes)

    def pool(fn, reads, writes):
        S.add("pool", fn, reads=reads, writes=writes)

    def pe(fn, reads, writes):
        S.add("pe", fn, reads=reads, writes=writes)

    ld = {"n": 0}

    def load_cast(dst, src, key):
        n = ld["n"]
        ld["n"] += 1
        S.add("pool", lambda e: e.dma_start(out=dst, in_=src), writes=[key], dsem=("mld", n % 8))

    def load_sp(dst, src, key):
        n = ld["n"]
        ld["n"] += 1
        S.add("sp", lambda e: e.dma_start(out=dst, in_=src), writes=[key], dsem=("mls", n % 8))

    winv = c.win[l].rearrange("p (a b) -> p a b", a=NKC)
    for ch in range(NKC):
        load_cast(WIN[:, ch, :], winv[:, ch, :], ("WIN", ch))
    load_cast(W1[:, 0:16, :], c.w1[l].rearrange("p (a b) -> p a b", b=128)[:, 0:16, :], ("W1", 0))
    load_cast(W1[:, 16:32, :], c.w1[l].rearrange("p (a b) -> p a b", b=128)[:, 16:32, :], ("W1", 1))
    load_cast(W2, c.w2[l].rearrange("p (a b) -> p a b", b=64), "W2")
    load_cast(PET, c.pet[l], "PET")
    load_cast(PW, c.pw[l].rearrange("p (a b) -> p a b", b=128), "PW")
    load_sp(PSC, c.psc[l], "PSC")
    for g in range(2):
        load_cast(KS[64:128, g, :], c.ckrow, ("KSc", g))
    load_cast(MSK, c.cmask.rearrange("m p j -> p m j"), "MSK")
    load_sp(AB, c.cab.rearrange("p (a b) -> p a b", a=8), "AB")
    load_sp(ABC, c.cabc.rearrange("p (a b c) -> p a b c", a=8, b=NCC), "ABC")
    load_sp(JR[64:65, :], c.cjr, "JR")
    load_cast(OV, c.cov.rearrange("p (a b) -> p a b", b=65), "OV")
    load_cast(SEL[0:24], c.csel.rearrange("p (a b) -> p a b", b=64), "SEL")
    load_sp(IC0, c.cic0.rearrange("p (a b) -> p a b", b=16), "IC0")
    wov = c.wout[l].rearrange("p (a b) -> p a b", a=NKC)
    for ch in range(NKC):
        load_cast(WO[:, ch, :], wov[:, ch, :], ("WO", ch))
    WINK = [("WIN", ch) for ch in range(NKC)]
    WOK = [("WO", ch) for ch in range(NKC)]
    pool(lambda e: e.memset(KW[64:128], 0.0), [], ["KWc"])
    pool(lambda e: e.memset(KW[64:65], 1.0), ["KWc"], ["KWc"])
    pool(lambda e: e.memset(KCC, 0.0), [], ["KCC"])
    pool(lambda e: e.memset(KCC[64:65], 1.0), ["KCC"], ["KCC"])
    pool(lambda e: e.memset(HK, 0.0), [], ["HK"])
    pool(lambda e: e.memset(HV, 0.0), [], ["HV"])
    pool(lambda e: e.memset(VS[:, :, :, 64:128], 1.0), [], ["VSc"])
    pool(lambda e: e.memset(VW[:, :, :, 64:128], 1.0), [], ["VWc"])
    pool(lambda e: e.memset(VCA, 0.0), [], ["VCA"])
    pool(lambda e: e.memset(VCA[:, :, :, 64:128], 1.0), ["VCA"], ["VCA"])
    pool(lambda e: e.memset(U, 0.0), [], ["U"])
    pool(lambda e: e.memset(KCT, 0.0), [], ["KCT"])
    pool(lambda e: e.memset(SM2[0], 0.0), [], [("SM", 0)])
    pool(lambda e: e.memset(SM2[1], 0.0), [], [("SM", 1)])
    pool(lambda e: e.memset(QA, 0.0), [], [("QA", h, x) for h in range(8) for x in "qm"])
    for h in range(8):
        pool(lambda e, h=h: e.tensor_scalar(out=QA[64:65, h, :], in0=JR[64:65, :], scalar1=-SLOPES[h], scalar2=None, op0=ALU.mult),
             ["JR", ("QA", h, "m")], [("QA", h, "m")])
    for kv in range(2):
        rows = slice(0, 64) if kv == 0 else slice(64, 128)
        for li in range(32):
            pe(lambda e, kv=kv, rows=rows, li=li: e.matmul(PS[6 - kv][:, 0:1], lhsT=W1[rows, li, :], rhs=PET[rows, li:li + 1], start=(li == 0), stop=(li == 31)),
               [("W1", li // 16), "PET"], [("ps", 6 - kv)])
    for kv in range(2):
        dve(lambda e, kv=kv: e.tensor_copy(out=CB[:, kv:kv + 1], in_=PS[6 - kv][:, 0:1]), [("ps", 6 - kv)], ["CB"])

    xTv = c.xT.rearrange("(c p) t -> p c t", p=128)
    wbs = [0, 1, 7]
    st = {"wb": 0, "pt": 0, "ev": 0}

    def next_wb():
        k = wbs[st["wb"] % 3]
        st["wb"] += 1
        return k

    def evac(fn_act, fn_dve, reads, writes):
        if os.environ.get("EVDVE"):
            dve(fn_dve, reads, writes)
        elif st["wb"] % 2 == 0:
            act(fn_act, reads, writes)
        else:
            dve(fn_dve, reads, writes)

    def proj(lhs_fn, M, rkeys):
        k = next_wb()
        for ch in range(NKC):
            pe(lambda e, ch=ch, k=k: e.matmul(PS[k][0:M, :], lhsT=lhs_fn(ch), rhs=xn[:, ch, :], start=(ch == 0), stop=(ch == NKC - 1)),
               [("WIN", ch), "xn"], [("ps", k)])
        return k

    abk = {"n": 0}
    pend = []
    LAGG = 3

    def pipe_pop():
        t, k, O, okey, first, last, done_cb = pend.pop(0)
        p = st["pt"] % NPT
        st["pt"] += 1
        act(lambda e, t=t, k=k, p=p: e.activation(out=PT[p], in_=PS[k], func=AF.Exp, bias=t["bias"]),
            [("ps", k)] + t["bk"], [("PT", p)])
        pe(lambda e, t=t, p=p: e.matmul(O, lhsT=t["v"], rhs=PT[p], start=first, stop=last),
           [("PT", p)] + t["vk"], [okey])
        if t.get("extra") is not None:
            t["extra"](PT[p], ("PT", p), first, last)
        if last and done_cb is not None:
            done_cb()

    def pipe_flush():
        while pend:
            pipe_pop()

    def attn(tiles, O, okey, banks=(0, 1, 7), done_cb=None):
        n = len(tiles)
        for i, t in enumerate(tiles):
            k = banks[abk["n"] % len(banks)]
            abk["n"] += 1
            pe(lambda e, t=t, k=k: e.matmul(PS[k], lhsT=t["lhsT"], rhs=t["rhs"], start=True, stop=(t["mask"] is None)),
               t["rk"], [("ps", k)])
            if t["mask"] is not None:
                pe(lambda e, t=t, k=k: e.matmul(PS[k], lhsT=c.identb, rhs=t["mask"], start=False, stop=True),
                   ["identb", "MSK"], [("ps", k)])
            pend.append((t, k, O, okey, i == 0, i == n - 1, done_cb))
            while len(pend) > min(LAGG, len(banks) - 1):
                pipe_pop()

    def gate_combine(h, br, Ops, okey, mode, hh):
        pe(lambda e: e.matmul(PS[6][0:64, :], lhsT=SEL[0:24, 3 * h + br, :], rhs=SG[0:24, :], start=True, stop=True),
           ["SEL", "SG"], [("ps", 6)])
        if br == 0:
            act(lambda e: e.activation(out=LM[0:64, :], in_=Ops[64:128, :], func=AF.Ln, bias=c.tinyt[0:64, :]), [okey, "tinyt"], ["LM"])
            act(lambda e: e.activation(out=LM[0:64, :], in_=LM[0:64, :], func=AF.Exp, scale=-1.0), ["LM"], ["LM"])
        else:
            dve(lambda e: e.reciprocal(out=LM[0:64, :], in_=Ops[64:128, :]), [okey], ["LM"])
        dve(lambda e: e.tensor_tensor(out=GR[0:64, :], in0=PS[6][0:64, :], in1=LM[0:64, :], op=ALU.mult),
            [("ps", 6), "LM"], ["GR"])
        pb = 64 * (hh // 4)
        hq = hh % 4
        accv = ACC[pb:pb + 64, hq, :]
        tmpv = TMP[pb:pb + 64, :]
        if mode == 0:
            dve(lambda e: e.tensor_tensor(out=accv, in0=Ops[0:64, :], in1=GR[0:64, :], op=ALU.mult),
                [okey, "GR"], [("ACC", hh)])
        else:
            dve(lambda e: e.tensor_tensor(out=tmpv, in0=Ops[0:64, :], in1=GR[0:64, :], op=ALU.mult),
                [okey, "GR", "LM"], ["LM"])
            if mode == 1:
                dve(lambda e: e.tensor_tensor(out=accv, in0=accv, in1=tmpv, op=ALU.add),
                    [("ACC", hh), "LM"], [("ACC", hh)])
            else:
                r0 = (h % 2) * 64
                dve(lambda e: e.tensor_tensor(out=ON[r0:r0 + 64, h // 2, :], in0=accv, in1=tmpv, op=ALU.add),
                    [("ACC", hh), "LM"], [("ON", h)])

    import os
    STOP = int(os.environ.get("MIXSTOP", "9"))
    SUB = os.environ.get("MIXSUB", "q,c,s,w,g,u,v").split(",")
    for G in range(NG):
        t0 = G * 512
        tkeys = [("xT", i) for i in range(t0 // 128, t0 // 128 + 4)]
        for ch in range(NKC):
            S.add("sp", lambda e, t0=t0, ch=ch: e.dma_start(out=xs[:, ch, :], in_=xTv[:, ch, t0:t0 + 512]), reads=tkeys, writes=[("xs", ch)], dsem=("mxs", ch))
        XSK = [("xs", ch) for ch in range(NKC)]
        if STOP < 1:
            S.add("sp", lambda e, t0=t0: e.dma_start(out=xTv[:, :, t0:t0 + 512], in_=xs), reads=XSK, writes=tkeys, dsem="mxwd")
            continue
        S.add("sp", lambda e, G=G: e.dma_start(out=FMg, in_=c.cfm[:, G * 256:(G + 1) * 256].rearrange("p (a b) -> p a b", b=64)), writes=["FMg"], dsem="mfm")
        norm_stats(c, xs, slice(0, 512), sq, PS[6], rstd, None, "mx", pskey=("ps", 6), kxs=XSK)
        for ch in range(NKC):
            dve(lambda e, ch=ch: e.scalar_tensor_tensor(out=xn[:, ch, :], in0=xs[:, ch, :], scalar=c.gamt[:, gcol + ch:gcol + ch + 1], in1=rstd, op0=ALU.mult, op1=ALU.mult),
                [("xs", ch), ("rstd", "mx"), "gam"], ["xn"])
        for j in range(4 if "q" in SUB else 0):
            k = proj(lambda ch, j=j: WIN[:, ch, j * 128:(j + 1) * 128], 128, None)
            for hp in range(2):
                h = 2 * j + hp
                src = PS[k][hp * 64:(hp + 1) * 64, :]
                evac(lambda e, h=h, src=src: e.activation(out=QA[0:64, h, :], in_=src, func=AF.Copy, scale=0.125),
                     lambda e, h=h, src=src: e.tensor_scalar(out=QA[0:64, h, :], in0=src, scalar1=0.125, scalar2=None, op0=ALU.mult),
                     [("ps", k)], [("QA", h, "q")])
        if G > 0:
            pool(lambda e: e.tensor_copy(out=KCT.rearrange("p g (s i) -> p g s i", i=33)[:, :, :, 0], in_=KCT.rearrange("p g (s i) -> p g s i", i=33)[:, :, :, 32]), ["KCT"], ["KCT"])
        for g in range(2 if "c" in SUB else 0):
            k = proj(lambda ch, g=g: WIN[:, ch, 512 + 128 * g:640 + 128 * g], 128, None)
            kdst = KCT[:, g, :].rearrange("p (s i) -> p i s", i=33)[:, 1:33, :]
            evac(lambda e, kdst=kdst, k=k: e.activation(out=kdst, in_=PS[k].rearrange("p (i s) -> p i s", s=16), func=AF.Copy),
                 lambda e, kdst=kdst, k=k: e.tensor_copy(out=kdst, in_=PS[k].rearrange("p (i s) -> p i s", s=16)),
                 [("ps", k), "KCT"], ["KCT"])
        k = proj(lambda ch: WIN[:, ch, 768:896], 128, None)
        for g in range(2 if "s" in SUB else 0):
            src = PS[k][g * 64:(g + 1) * 64, :]
            evac(lambda e, g=g, src=src, t0=t0: e.activation(out=KS[0:64, g, t0:t0 + 512], in_=src, func=AF.Copy),
                 lambda e, g=g, src=src, t0=t0: e.tensor_copy(out=KS[0:64, g, t0:t0 + 512], in_=src),
                 [("ps", k)], [("KS", g, G)])
        k = proj(lambda ch: WIN[:, ch, 1152:1280], 128, None)
        wc0 = (G % 2) * 512
        for g in range(2 if "w" in SUB else 0):
            src = PS[k][g * 64:(g + 1) * 64, :]
            evac(lambda e, g=g, src=src, wc0=wc0: e.activation(out=KW[0:64, g, wc0:wc0 + 512], in_=src, func=AF.Copy),
                 lambda e, g=g, src=src, wc0=wc0: e.tensor_copy(out=KW[0:64, g, wc0:wc0 + 512], in_=src),
                 [("ps", k)], [("KW", g, G % 2)])
        k = proj(lambda ch: WIN[:, ch, 1280:1304], 24, None) if "g" in SUB else next_wb()
        if "g" in SUB:
          act(lambda e, k=k: e.activation(out=SG[0:24, :], in_=PS[k][0:24, :], func=AF.Sigmoid), [("ps", k)], ["SG"])
        if G > 0:
            pool(lambda e: e.tensor_copy(out=U[:, :, 0:16], in_=U[:, :, 512:528]), ["U"], ["U"])
        for j in range(4 if "u" in SUB else 0):
            k = proj(lambda ch, j=j: WIN[:, ch, 1304 + 128 * j:1304 + 128 * (j + 1)], 128, None)
            evac(lambda e, j=j, k=k: e.activation(out=U[:, j, 16:528], in_=PS[k], func=AF.Copy),
                 lambda e, j=j, k=k: e.tensor_copy(out=U[:, j, 16:528], in_=PS[k]),
                 [("ps", k), "U"], ["U"])
        for tt in range(4 if ("v" in SUB or "vm" in SUB or "v1" in SUB or "v2" in SUB) else 0):
            k = next_wb()
            kc = 4 * G + tt
            for ch in range(NKC):
                pe(lambda e, ch=ch, k=k, tt=tt: e.matmul(PS[k][:, 0:256], lhsT=xn[:, ch, tt * 128:(tt + 1) * 128],
                                                          rhs=WIN[:, ch, 896:1152],
                                                          start=(ch == 0), stop=(ch == NKC - 1)),
                   [("WIN", ch), "xn"], [("ps", k)])
            if "vm" in SUB or "v2" in SUB:
                pass
            else:
              evac(lambda e, k=k, kc=kc: e.activation(out=VS[:, kc, :, 0:64], in_=PS[k][:, 0:128].rearrange("p (g d) -> p g d", d=64), func=AF.Copy),
                 lambda e, k=k, kc=kc: e.tensor_copy(out=VS[:, kc, :, 0:64], in_=PS[k][:, 0:128].rearrange("p (g d) -> p g d", d=64)),
                 [("ps", k), "VSc"], [("VS", kc)])
            if "vm" in SUB or "v1" in SUB:
                pass
            else:
              evac(lambda e, k=k, kc=kc: e.activation(out=VW[:, kc % 8, :, 0:64], in_=PS[k][:, 128:256].rearrange("p (g d) -> p g d", d=64), func=AF.Copy),
                 lambda e, k=k, kc=kc: e.tensor_copy(out=VW[:, kc % 8, :, 0:64], in_=PS[k][:, 128:256].rearrange("p (g d) -> p g d", d=64)),
                 [("ps", k), "VWc"], [("VW", kc % 8)])
        if STOP < 2:
            S.add("sp", lambda e, t0=t0: e.dma_start(out=xTv[:, :, t0:t0 + 512], in_=xs), reads=XSK, writes=tkeys, dsem="mxwd")
            continue
        pc0 = 32 * G
        khs = [next_wb(), next_wb()]
        for kv in [int(x) for x in os.environ.get("KVS", "0,1").split(",")]:
            kh = khs[kv]
            rows = slice(0, 64) if kv == 0 else slice(64, 128)
            for g in range(int(os.environ.get("NGG", "2"))):
                for li in range(int(os.environ.get("NLI", "32"))):
                    rhs = KCT[rows, g, li * 33:li * 33 + 32] if li < 16 else KCT[rows, g, (li - 16) * 33 + 1:(li - 16) * 33 + 33]
                    pe(lambda e, kv=kv, g=g, li=li, rhs=rhs, rows=rows, kh=kh: e.matmul(PS[kh][:, g * 32:(g + 1) * 32], lhsT=W1[rows, li, :], rhs=rhs, start=(li == 0), stop=(li == int(os.environ.get("NLI", "32")) - 1)),
                       [("W1", li // 16), "KCT"], [("ps", kh)])
        CSUB = os.environ.get("CSUB", "abc")
        for kv in range(2 if "b" in CSUB else 0):
            Hd = HK if kv == 0 else HV
            hkey = "HK" if kv == 0 else "HV"
            gx = GX[:, kv * 2:kv * 2 + 2, :]
            gt = GT[:, kv * 2:kv * 2 + 2, :]
            src = PS[khs[kv]][:, 0:64].rearrange("p (g i) -> p g i", i=32)
            act(lambda e, kv=kv, gx=gx, src=src: e.activation(out=gx, in_=src, func=AF.Identity, bias=CB[:, kv:kv + 1]),
                [("ps", khs[kv]), "CB"], [("GX", kv)])
            dve(lambda e, gx=gx, gt=gt: e.tensor_tensor(out=gt, in0=gx, in1=gx, op=ALU.mult), [("GX", kv)], [("GT", kv)])
            dve(lambda e, gt=gt: e.tensor_scalar(out=gt, in0=gt, scalar1=0.044715, scalar2=1.0, op0=ALU.mult, op1=ALU.add), [("GT", kv)], [("GT", kv)])
            dve(lambda e, gx=gx, gt=gt: e.tensor_tensor(out=gt, in0=gt, in1=gx, op=ALU.mult), [("GT", kv), ("GX", kv)], [("GT", kv)])
            act(lambda e, gt=gt: e.activation(out=gt, in_=gt, func=AF.Sigmoid, scale=2.0 * 0.7978845608028654), [("GT", kv)], [("GT", kv)])
            dve(lambda e, gx=gx, gt=gt, Hd=Hd, pc0=pc0: e.tensor_tensor(out=Hd[:, :, pc0:pc0 + 32], in0=gx, in1=gt, op=ALU.mult),
                [("GT", kv), ("GX", kv)], [hkey])
        k = next_wb()
        for g in range(2 if "c" in CSUB else 0):
            pe(lambda e, g=g, k=k, pc0=pc0: e.matmul(PS[k][0:64, g * 32:(g + 1) * 32], lhsT=W2[:, 0, :], rhs=HK[:, g, pc0:pc0 + 32], start=True, stop=True),
               ["W2", "HK"], [("ps", k)])
        dve(lambda e, k=k, pc0=pc0: e.tensor_copy(out=KCC[0:64, :, pc0:pc0 + 32], in_=PS[k][0:64, 0:64].rearrange("p (g i) -> p g i", i=32)),
            [("ps", k), "KCC"], ["KCC"])
        ccn = G // 4
        k = next_wb()
        for g in range(2 if "c" in CSUB else 0):
            pe(lambda e, g=g, k=k, ccn=ccn: e.matmul(PS[k][:, g * 64:(g + 1) * 64], lhsT=HV[:, g, ccn * 128:(ccn + 1) * 128], rhs=W2[:, 1, :], start=True, stop=True),
               ["W2", "HV"], [("ps", k)])
        dve(lambda e, k=k, ccn=ccn: e.tensor_copy(out=VCA[:, ccn, :, 0:64], in_=PS[k][:, 0:128].rearrange("p (g d) -> p g d", d=64)),
            [("ps", k), "VCA"], ["VCA"])

        if STOP < 3:
            S.add("sp", lambda e, t0=t0: e.dma_start(out=xTv[:, :, t0:t0 + 512], in_=xs), reads=XSK, writes=tkeys, dsem="mxwd")
            continue
        ncc = G // 4 + 1

        def do_cmp(g):
            IAg = IA2[g]
            dve(lambda e: e.memset(IAg, 0.0), [], [("IA", g)])
            for hh in range(4):
                h = 4 * g + hh
                tiles = []
                for cc in range(ncc):
                    Dd = G - 4 * cc

                    def extra(pt, ptk, first, last, cc=cc):
                        for qt in range(4):
                            pe(lambda e, qt=qt, pt=pt, cc=cc: e.matmul(PS[5][:, qt * 65:(qt + 1) * 65], lhsT=pt[:, qt * 128:(qt + 1) * 128], rhs=OV[:, cc, :], start=(first and qt == 0), stop=last, skip_group_check=True),
                               [ptk, "OV"], [("ps", 5)])
                    tiles.append(dict(lhsT=KCC[0:65, g, cc * 128:(cc + 1) * 128], rhs=QA[0:65, h, :], rk=["KCC", ("QA", h, "q"), ("QA", h, "m")],
                                      mask=(MSK[:, 8 + Dd, :] if Dd <= 3 else None), bias=ABC[:, h, cc, G:G + 1], bk=["ABC"],
                                      v=VCA[:, cc, g, :], vk=["VCA"], extra=extra))
                cb = (2, 3, 4)[hh % 3]

                def after_cmp(h=h, hh=hh, cb=cb):
                    imv = PS[5][:, 0:260].rearrange("p (q n) -> p q n", n=65)
                    dve(lambda e, imv=imv: e.tensor_scalar(out=RL, in0=imv[:, :, 64], scalar1=1e-30, scalar2=None, op0=ALU.max), [("ps", 5)], ["RL"])
                    dve(lambda e: e.reciprocal(out=RL, in_=RL), ["RL"], ["RL"])
                    for qt in range(4):
                        dve(lambda e, qt=qt, imv=imv, IAg=IAg: e.scalar_tensor_tensor(out=IAg[:, qt, :], in0=imv[:, qt, 0:64], scalar=RL[:, qt:qt + 1], in1=IAg[:, qt, :], op0=ALU.mult, op1=ALU.add),
                            [("ps", 5), "RL", ("IA", g)], [("IA", g)])
                    gate_combine(h, 0, PS[cb], ("ps", cb), 0, 4 * g + hh)
                attn(tiles, PS[cb], ("ps", cb), done_cb=after_cmp)
            pipe_flush()

        def do_topk(g):
            IAg = IA2[g]
            SMg = SM2[g]
            dve(lambda e: e.tensor_tensor(out=IAg, in0=IAg, in1=FMg, op=ALU.add), [("IA", g), "FMg"], [("IA", g)])
            for qt in range(4):
                dve(lambda e, qt=qt: e.max(out=IMX[:, qt, :], in_=IAg[:, qt, :]), [("IA", g)], ["IMX"])
            for qt in range(4):
                dve(lambda e, qt=qt: e.tensor_scalar(out=SMg[:, qt, 64:128], in0=IAg[:, qt, :], scalar1=IMX[:, qt, 7:8], scalar2=1.0, op0=ALU.is_ge, op1=ALU.subtract),
                    [("IA", g), "IMX", ("SM", g)], [("SM", g)])
            for qt in range(4):
                pe(lambda e, qt=qt: e.matmul(PS[5][:, qt * 128:(qt + 1) * 128], lhsT=SMg[:, qt, :], rhs=c.identb, start=True, stop=True),
                   [("SM", g), "identb"], [("ps", 5)])
            for hh in range(4):
                h = 4 * g + hh
                dve(lambda e, h=h: e.tensor_copy(out=QA[64:128, h, :], in_=PS[5][64:128, :]),
                    [("ps", 5)], [("QA", h, "m")])
                dve(lambda e, h=h: e.tensor_scalar(out=QA[64:65, h, :], in0=JR[64:65, :], scalar1=-SLOPES[h], scalar2=None, op0=ALU.mult),
                    ["JR", ("QA", h, "m")], [("QA", h, "m")])

        def do_head(g, hh):
            h = 4 * g + hh
            tiles = []
            for kc in range(4 * G + 4):
                r = kc - 4 * G
                tiles.append(dict(lhsT=KS[:, g, kc * 128:(kc + 1) * 128], rhs=QA[:, h, :], rk=[("KSc", g), ("KS", g, kc // 4), ("QA", h, "q"), ("QA", h, "m")],
                                  mask=(MSK[:, r, :] if r >= 0 else None), bias=AB[:, h, r + NT - 4:r + NT - 3], bk=["AB"],
                                  v=VS[:, kc, g, :], vk=[("VS", kc), "VSc"]))
            attn(tiles, PS[3], ("ps", 3), banks=(0, 1, 7, 2), done_cb=lambda h=h, g=g, hh=hh: gate_combine(h, 1, PS[3], ("ps", 3), 1, 4 * g + hh))
            tiles = []
            for kc in range(max(0, 4 * G - 4), 4 * G + 4):
                r = kc - 4 * G
                mi = r if r >= 0 else 8 + r
                sl = kc % 8
                tiles.append(dict(lhsT=KW[:, g, sl * 128:(sl + 1) * 128], rhs=QA[:, h, :], rk=["KWc", ("KW", g, (kc // 4) % 2), ("QA", h, "q"), ("QA", h, "m")],
                                  mask=MSK[:, mi, :], bias=AB[:, h, r + NT - 4:r + NT - 3], bk=["AB"],
                                  v=VW[:, sl, g, :], vk=[("VW", sl), "VWc"]))
            attn(tiles, PS[4], ("ps", 4), banks=(0, 1, 7, 2), done_cb=lambda h=h, g=g, hh=hh: gate_combine(h, 2, PS[4], ("ps", 4), 2, 4 * g + hh))

        do_cmp(0)
        do_cmp(1)
        do_topk(0)
        do_head(0, 0)
        do_topk(1)
        for hh in range(1, 4):
            do_head(0, hh)
        for hh in range(4):
            do_head(1, hh)
        pipe_flush()
        if STOP < 6:
            S.add("sp", lambda e, t0=t0: e.dma_start(out=xTv[:, :, t0:t0 + 512], in_=xs), reads=XSK, writes=tkeys, dsem="mxwd")
            continue
        srcs = [None] * 4
        for (ja, jb) in ((0, 2), (2, 4)):
            cur, base = U, 0
            for kk in range(jb):
                sh = 1 << kk
                lo = max(kk, ja)
                dst = UW[kk % 2]
                dve(lambda e, cur=cur, base=base, dst=dst, sh=sh, lo=lo, ja=ja, jb=jb: e.tensor_tensor(out=dst[:, lo - ja:jb - ja, sh:528], in0=cur[:, lo - base:jb - base, sh:528], in1=cur[:, lo - base:jb - base, 0:528 - sh], op=ALU.add),
                    ["U", "UW0", "UW1"], ["UW%d" % (kk % 2)])
                if ja <= kk < jb:
                    srcs[kk] = dst[:, kk - ja, :]
                cur, base = dst, ja
            for j in range(ja, jb):
                w = POOL_W[j]
                src = srcs[j]
                if G == 0:
                    dve(lambda e, src=src, j=j: e.tensor_tensor(out=src[:, 16:32], in0=src[:, 16:32], in1=IC0[:, j, :], op=ALU.mult),
                        ["UW0", "UW1", "IC0"], ["UW%d" % (j % 2)])
                dve(lambda e, src=src, j=j, w=w: e.scalar_tensor_tensor(out=Y2[j % 2], in0=src[:, 16:528], scalar=1.0 / w, in1=U[:, j, 16:528], op0=ALU.mult, op1=ALU.subtract),
                    ["UW0", "UW1", "U"], [("Y", j % 2)])
                k = next_wb()
                pe(lambda e, j=j, k=k: e.matmul(PS[k], lhsT=PW[:, j, :], rhs=Y2[j % 2], start=True, stop=True), ["PW", ("Y", j % 2)], [("ps", k)])
                dve(lambda e, j=j, k=k: e.tensor_scalar(out=OPt[:, j, :], in0=PS[k], scalar1=PSC[:, j:j + 1], scalar2=None, op0=ALU.mult),
                    [("ps", k), "PSC"], [("OP", j)])
        for o in range(NKC):
            k = next_wb()
            for j in range(8):
                rhs = ON[:, j, :] if j < 4 else OPt[:, j - 4, :]
                rk = [("ON", 2 * j), ("ON", 2 * j + 1)] if j < 4 else [("OP", j - 4)]
                pe(lambda e, o=o, j=j, k=k, rhs=rhs: e.matmul(PS[k], lhsT=WO[:, j, o * 128:(o + 1) * 128], rhs=rhs, start=(j == 0), stop=(j == 7)),
                   [("WO", j)] + rk, [("ps", k)])
            dve(lambda e, o=o, k=k: e.tensor_tensor(out=xs[:, o, :], in0=PS[k], in1=xs[:, o, :], op=ALU.add),
                [("ps", k), ("xs", o)], [("xs", o)])
            S.add("sp", lambda e, t0=t0, o=o: e.dma_start(out=xTv[:, o, t0:t0 + 512], in_=xs[:, o, :]),
                  reads=[("xs", o)], writes=[("xTc", G, o)], dsem=("mxw", o))
        S.add("sp", lambda e: e.nop(), reads=[("xTc", G, o) for o in range(NKC)], writes=tkeys)


def host_pack(inp, L, with_mixer=True):
    f32 = np.float32
    wgu = np.empty((L, 2, NF, 128, 2, NKC, 128), f32)
    wd = np.empty((L, 2, NKC, 128, NF, 128), f32)
    ffn_w = ((inp["ffn1_wg"], inp["ffn1_wu"], inp["ffn1_wd"]), (inp["ffn2_wg"], inp["ffn2_wu"], inp["ffn2_wd"]))
    for l in range(L):
        for w in range(2):
            for j in range(2):
                a = np.asarray(ffn_w[w][j][l], f32).reshape(NKC, 128, NF, 128)
                wgu[l, w, :, :, j] = a.transpose(2, 1, 0, 3)
            a = np.asarray(ffn_w[w][2][l], f32).reshape(NF, 128, NKC, 128)
            wd[l, w] = a.transpose(2, 1, 0, 3)
    gam = np.empty((128, L * 24 + 8), f32)
    for l in range(L):
        for j, nm in enumerate(("ffn1_norm", "mix_norm", "ffn2_norm")):
            gam[:, l * 24 + j * 8:l * 24 + j * 8 + 8] = np.asarray(inp[nm][l], f32).reshape(8, 128).T
    gam[:, L * 24:] = np.asarray(inp["final_norm"], f32).reshape(8, 128).T
    T = int(np.asarray(inp["x"]).shape[1])
    NT, NG = T // 128, T // 512
    NCC = (NG + 3) // 4
    g = lambda k: np.asarray(inp[k], f32)
    perm = np.concatenate([np.arange(0, 512), np.arange(512, 576), np.arange(640, 704), np.arange(576, 640), np.arange(704, 768),
                           np.arange(768, 1024), np.arange(1152, 1280), np.arange(1024, 1152), np.arange(1280, INW)])
    win = g("w_in")[:, :, perm].reshape(L, NKC, 128, INW).transpose(0, 2, 1, 3).reshape(L, 128, NKC * INW)
    wout = g("w_out").reshape(L, NKC, 128, D).transpose(0, 2, 1, 3).reshape(L, 128, NKC * D)
    w1 = np.concatenate([g("cmp_wk1").reshape(L, 32, 64, 128).transpose(0, 2, 1, 3),
                         g("cmp_wv1").reshape(L, 32, 64, 128).transpose(0, 2, 1, 3)], axis=1).reshape(L, 128, 32 * 128)
    w2 = np.stack([g("cmp_wk2"), g("cmp_wv2")], axis=2).reshape(L, 128, 128)
    pet = np.concatenate([g("cmp_pe_k").transpose(0, 2, 1), g("cmp_pe_v").transpose(0, 2, 1)], axis=1)
    pw = g("pool_w").transpose(0, 2, 1, 3).reshape(L, 128, 512)
    psc = g("pool_scale").reshape(L, 4, 128).transpose(0, 2, 1)
    p = np.arange(128)[:, None]
    j = np.arange(512)[None, :]
    ckrow = np.zeros((64, T), f32)
    ckrow[0, :] = 1.0
    key = np.arange(T)
    for r in range(1, 64):
        ckrow[r, key // 64 == r] = BIG
    masks = np.zeros((12, 128, 512), f32)
    for r in range(4):
        masks[r] = np.where(128 * r + p <= j, 0.0, -BIG)
        masks[4 + r] = np.where(j < 512 + 128 * (r - 4) + p, 0.0, -BIG)
        masks[8 + r] = np.where(16 * p + 15 - 512 * r <= j, 0.0, -BIG)
    sl = np.asarray(SLOPES, f32)
    cab = np.zeros((128, 8, NT), f32)
    for i in range(NT):
        rel = i - (NT - 4)
        cab[:, :, i] = sl[None, :] * (128.0 * rel + p - 256.0)
    cabc = np.zeros((128, 8, NCC, NG), f32)
    for cc in range(NCC):
        for G in range(NG):
            cabc[:, :, cc, G] = sl[None, :] * (2048.0 * cc + 16.0 * p + 15.0 - 512.0 * G - 256.0)
    cabc[0, :, 0, :] = -1e30
    cjr = (np.arange(512, dtype=f32) - 256.0)[None, :]
    cfm = np.zeros((128, NT, 64), f32)
    n = np.arange(64)[None, :]
    for a in range(NT):
        cur = (128 * a + p) // 64
        cfm[:, a, :] = np.where((n == 0) | (n == cur) | (n == cur - 1), 8.0, 0.0)
    cov = np.zeros((128, NCC, 65), f32)
    for cc in range(NCC):
        pc = 128 * cc + p
        cblk = pc - 1
        ov = (16 * cblk <= 64 * n + 63) & (16 * cblk + 31 >= 64 * n) & (pc >= 1)
        cov[:, cc, 0:64] = ov
        cov[:, cc, 64] = 1.0
    csel = np.zeros((24, 24, 64), f32)
    for k in range(24):
        csel[k, k, :] = 1.0
    cic0 = np.zeros((128, 4, 16), f32)
    for jj, w in enumerate(POOL_W):
        t = np.arange(16)
        cic0[:, jj, :] = (w / np.minimum(t + 1, w))[None, :]
    m = {
        "win": np.ascontiguousarray(win), "wout": np.ascontiguousarray(wout), "w1": np.ascontiguousarray(w1),
        "w2": np.ascontiguousarray(w2), "pet": np.ascontiguousarray(pet), "pw": np.ascontiguousarray(pw),
        "psc": np.ascontiguousarray(psc), "ckrow": ckrow, "cmask": masks,
        "cab": cab.reshape(128, 8 * NT), "cabc": cabc.reshape(128, 8 * NCC * NG), "cjr": cjr,
        "cfm": cfm.reshape(128, NT * 64), "cov": cov.reshape(128, NCC * 65), "csel": csel.reshape(24, 24 * 64),
        "cic0": cic0.reshape(128, 64),
        "wgu": wgu.reshape(L, 2, NF, 128, 2 * NKC * 128),
        "wd": wd.reshape(L, 2, NKC, 128, NF * 128),
        "gam": gam,
        "cident": np.eye(128, dtype=f32),
    }
    if not with_mixer:
        for k in ("win", "wout", "w1", "w2", "pet", "pw", "psc", "ckrow", "cmask", "cab", "cabc", "cjr", "cfm", "cov", "csel", "cic0"):
            m.pop(k)
    return m


_CACHE = {}
WITH_MIXER = True


def kernel(**inputs):
    x = np.asarray(inputs["x"], np.float32)
    B, T, _ = x.shape
    L = int(np.asarray(inputs["ffn1_wg"]).shape[0])
    key = (T, L)
    if key not in _CACHE:
        _CACHE[key] = build(T, L, WITH_MIXER)
    nc = _CACHE[key]
    shared = host_pack(inputs, L, WITH_MIXER)
    in_maps = []
    for b in range(B):
        m = dict(shared)
        m["x"] = np.ascontiguousarray(x[b])
        in_maps.append(m)
    res = run_bass_kernel_spmd(nc, in_maps, core_ids=list(range(B)))
    return np.stack([np.asarray(r["out"], np.float32) for r in res.results], axis=0)
```
